# Optimizing a Trainium2 kernel written in Bass

```python
import math
import jax, jax.numpy as jnp
from jax import lax
import numpy as np

D_MODEL = 1024
BATCH = 16
SEQ = 4096
DEPTH = 4

GRID_W = 64
CTX_LEN = 256
N_MIXERS = 2
N_ATTN_LAYERS = (DEPTH + 1) // 2
N_HGRN_LAYERS = DEPTH // 2
EPS = 1e-6

HEAD_DIM = 64
N_HEADS = D_MODEL // HEAD_DIM
N_KV_HEADS = 4
GROUP = N_HEADS // N_KV_HEADS
WINDOW = 128
BLOCK = 128
ROPE_THETA = 10000.0
ATTN_IN_W = (N_HEADS + 2 * N_KV_HEADS) * HEAD_DIM

HG_EXPAND = 128
HG_HEADS = D_MODEL // HG_EXPAND
HG_DK = HG_EXPAND
HG_DV = D_MODEL // HG_HEADS
HG_DF = HG_HEADS * HG_DK
HG_CHUNK = 32
HG_SPLITS = [HG_DF, 2 * HG_DF, 3 * HG_DF, 3 * HG_DF + D_MODEL]
HG_IN_W = 3 * HG_DF + 2 * D_MODEL

D_FF = 2816
CONV_W = 3

kernel_name = 'hybrid_swa_hgrn2_convglu_dit'


def rms_norm(x, gain):
    xf = x.astype(jnp.float32)
    y = xf * lax.rsqrt(jnp.mean(xf * xf, axis=-1, keepdims=True) + EPS)
    return (y * gain.astype(jnp.float32)).astype(x.dtype)


def axial_angles(rows):
    row = jnp.repeat(jnp.arange(rows, dtype=jnp.float32), GRID_W)
    col = jnp.tile(jnp.arange(GRID_W, dtype=jnp.float32), rows)
    n_pairs = HEAD_DIM // 4
    inv = ROPE_THETA ** (-jnp.arange(n_pairs, dtype=jnp.float32) / n_pairs)
    return row[:, None] * inv, col[:, None] * inv


def rope_rotate(x, ang):
    x1, x2 = jnp.split(x, 2, axis=-1)
    cos = jnp.cos(ang)[None, :, None, :].astype(x.dtype)
    sin = jnp.sin(ang)[None, :, None, :].astype(x.dtype)
    return jnp.concatenate([x1 * cos - x2 * sin, x1 * sin + x2 * cos], axis=-1)


def apply_axial_rope(x, ang_r, ang_c):
    xr, xc = jnp.split(x, 2, axis=-1)
    return jnp.concatenate([rope_rotate(xr, ang_r), rope_rotate(xc, ang_c)], axis=-1)


def sink_softmax(scores, sink_g):
    s_sink = jnp.broadcast_to(sink_g[None, :, :, None, None].astype(jnp.float32), scores.shape[:-1] + (1,))
    p = jax.nn.softmax(jnp.concatenate([scores, s_sink], axis=-1), axis=-1)
    return p[..., :-1]


def attn_project(h, w_in, q_gain, k_gain):
    b, t, _ = h.shape
    q, k, v = jnp.split(h @ w_in, [N_HEADS * HEAD_DIM, (N_HEADS + N_KV_HEADS) * HEAD_DIM], axis=-1)
    q = rms_norm(q.reshape(b, t, N_HEADS, HEAD_DIM), q_gain)
    k = rms_norm(k.reshape(b, t, N_KV_HEADS, HEAD_DIM), k_gain)
    v = v.reshape(b, t, N_KV_HEADS, HEAD_DIM)
    return q, k, v


def windowed_gqa(hx, hc, w_in, w_out, q_gain, k_gain, sink, ang_r, ang_c, need_ctx_out):
    B, L, _ = hx.shape
    nb = L // BLOCK
    scale = HEAD_DIM ** -0.5
    qx, kx, vx = attn_project(hx, w_in, q_gain, k_gain)
    qc, kc, vc = attn_project(hc, w_in, q_gain, k_gain)
    qx = apply_axial_rope(qx, ang_r, ang_c)
    kx = apply_axial_rope(kx, ang_r, ang_c)
    kc_t = kc.transpose(0, 2, 1, 3)
    vc_t = vc.transpose(0, 2, 1, 3)
    pad = ((0, 0), (0, 0), (BLOCK, BLOCK), (0, 0))
    kx_p = jnp.pad(kx.transpose(0, 2, 1, 3), pad)
    vx_p = jnp.pad(vx.transpose(0, 2, 1, 3), pad)
    q_blocks = qx.reshape(B, nb, BLOCK, N_KV_HEADS, GROUP, HEAD_DIM).transpose(1, 0, 3, 4, 2, 5)
    sink_g = sink.reshape(N_KV_HEADS, GROUP)
    q_off = jnp.arange(BLOCK, dtype=jnp.int32)
    k_off = jnp.arange(3 * BLOCK, dtype=jnp.int32) - BLOCK

    def block(args):
        n, qb = args
        start = n * BLOCK
        kb = lax.dynamic_slice_in_dim(kx_p, start, 3 * BLOCK, axis=2)
        vb = lax.dynamic_slice_in_dim(vx_p, start, 3 * BLOCK, axis=2)
        qi = start + q_off
        kj = start + k_off
        valid = ((kj >= 0) & (kj < L))[None, :] & (jnp.abs(qi[:, None] - kj[None, :]) <= WINDOW)
        s_win = jnp.einsum('bkgqd,bkjd->bkgqj', qb, kb).astype(jnp.float32) * scale
        s_win = jnp.where(valid, s_win, -jnp.inf)
        s_ctx = jnp.einsum('bkgqd,bkcd->bkgqc', qb, kc_t).astype(jnp.float32) * scale
        p = sink_softmax(jnp.concatenate([s_win, s_ctx], axis=-1), sink_g).astype(vb.dtype)
        return (jnp.einsum('bkgqj,bkjd->bkgqd', p[..., :3 * BLOCK], vb)
                + jnp.einsum('bkgqc,bkcd->bkgqd', p[..., 3 * BLOCK:], vc_t))

    o = lax.map(block, (jnp.arange(nb, dtype=jnp.int32), q_blocks))
    ox = o.transpose(1, 0, 4, 2, 3, 5).reshape(B, L, N_HEADS * HEAD_DIM)
    yc = None
    if need_ctx_out:
        C = hc.shape[1]
        qcg = qc.reshape(B, C, N_KV_HEADS, GROUP, HEAD_DIM).transpose(0, 2, 3, 1, 4)
        s = jnp.einsum('bkgqd,bkcd->bkgqc', qcg, kc_t).astype(jnp.float32) * scale
        p = sink_softmax(s, sink_g).astype(vc.dtype)
        oc = jnp.einsum('bkgqc,bkcd->bkgqd', p, vc_t).transpose(0, 3, 1, 2, 4).reshape(B, C, N_HEADS * HEAD_DIM)
        yc = oc @ w_out
    return ox @ w_out, yc


def chunk_scan(q, k, g, v, s0, need_out):
    B, T, H, dk = q.shape
    dv = v.shape[-1]
    n = T // HG_CHUNK

    def to_chunks(a):
        return a.reshape(B, n, HG_CHUNK, H, a.shape[-1]).transpose(1, 0, 3, 2, 4).astype(jnp.float32)

    tri = jnp.tril(jnp.ones((HG_CHUNK, HG_CHUNK), dtype=bool))

    def step(S, inp):
        qc, kc, gc, vc = inp
        b = jnp.cumsum(gc, axis=2)
        b_last = b[:, :, -1:, :]
        k_dec = kc * jnp.exp(b_last - b)
        S_new = jnp.exp(b_last[:, :, 0, :])[..., None] * S + jnp.einsum('bhck,bhcv->bhkv', k_dec, vc)
        if not need_out:
            return S_new, None
        o_inter = jnp.einsum('bhck,bhkv->bhcv', qc * jnp.exp(b), S)
        diff = jnp.where(tri[:, :, None], b[:, :, :, None, :] - b[:, :, None, :, :], -jnp.inf)
        a = jnp.einsum('bhtk,bhsk,bhtsk->bhts', qc, kc, jnp.exp(diff))
        return S_new, o_inter + jnp.einsum('bhts,bhsv->bhtv', a, vc)

    S_fin, o = lax.scan(step, s0, (to_chunks(q), to_chunks(k), to_chunks(g), to_chunks(v)))
    if need_out:
        o = o.transpose(1, 0, 3, 2, 4).reshape(B, T, H, dv)
    return o, S_fin


def hgrn2_bidir(hx, hc, w_in, w_out, o_gain, lb, need_ctx_out):
    def project(h):
        b, t, _ = h.shape
        q, f_fw, f_bw, inp, gate = jnp.split(h @ w_in, HG_SPLITS, axis=-1)
        q = jax.nn.silu(q).reshape(b, t, HG_HEADS, HG_DK)
        inp = inp.reshape(b, t, HG_HEADS, HG_DV)

        def forget(fl):
            f = lb + (1.0 - lb) * jax.nn.sigmoid(fl.astype(jnp.float32))
            return (1.0 - f).reshape(b, t, HG_HEADS, HG_DK), jnp.log(f).reshape(b, t, HG_HEADS, HG_DK)

        return q, forget(f_fw), forget(f_bw), inp, gate

    flip = lambda a: jnp.flip(a, axis=1)
    qx, (kxf, gxf), (kxb, gxb), vx, gx = project(hx)
    qc, (kcf, gcf), (kcb, gcb), vc, gc = project(hc)
    s0 = jnp.zeros((hc.shape[0], HG_HEADS, HG_DK, HG_DV), jnp.float32)
    oc_f, sc_f = chunk_scan(qc, kcf, gcf, vc, s0, need_ctx_out)
    oc_b, sc_b = chunk_scan(flip(qc), flip(kcb), flip(gcb), flip(vc), s0, need_ctx_out)
    ox_f, _ = chunk_scan(qx, kxf, gxf, vx, sc_f, True)
    ox_b, _ = chunk_scan(flip(qx), flip(kxb), flip(gxb), flip(vx), sc_b, True)

    def readout(o, gate, h):
        o = rms_norm(o, o_gain).astype(h.dtype) * jax.nn.silu(gate).reshape(o.shape)
        return o.reshape(h.shape[0], h.shape[1], D_MODEL) @ w_out

    yx = readout(ox_f + flip(ox_b), gx, hx)
    yc = readout(oc_f + flip(oc_b), gc, hc) if need_ctx_out else None
    return yx, yc


def dwconv3(u, w, b):
    up = jnp.pad(u, ((0, 0), (1, 1), (0, 0)))
    return up[:, :-2] * w[0] + up[:, 1:-1] * w[1] + up[:, 2:] * w[2] + b


def conv_glu(h, w_up, conv_w, conv_b, w_down):
    gate, val = jnp.split(h @ w_up, 2, axis=-1)
    return (jax.nn.silu(dwconv3(gate, conv_w, conv_b)) * val) @ w_down


def setup_inputs(seed: int = 0) -> dict:
    key = jax.random.key(seed)
    ks = jax.random.split(key, 21)
    D = D_MODEL

    def nrm(k, shape, scale):
        return jax.random.normal(k, shape, jnp.float32) * scale

    def gain(k, shape):
        return 1.0 + 0.05 * jax.random.normal(k, shape, jnp.float32)

    return {
        'x': nrm(ks[0], (BATCH, SEQ, D), 1.0),
        'c': nrm(ks[1], (BATCH, D), 1.0),
        'ctx': nrm(ks[2], (BATCH, CTX_LEN, D), 1.0),
        'c_ctx': nrm(ks[3], (D,), 1.0),
        'ada_w': nrm(ks[4], (DEPTH, D, 6 * D), 0.5 * D ** -0.5),
        'ada_b': nrm(ks[5], (DEPTH, 6 * D), 0.02),
        'norm1_g': gain(ks[6], (DEPTH, D)),
        'norm2_g': gain(ks[7], (DEPTH, D)),
        'attn_w_in': nrm(ks[8], (N_ATTN_LAYERS, D, ATTN_IN_W), D ** -0.5),
        'attn_w_out': nrm(ks[9], (N_ATTN_LAYERS, N_HEADS * HEAD_DIM, D), (N_HEADS * HEAD_DIM) ** -0.5),
        'attn_q_gain': gain(ks[10], (N_ATTN_LAYERS, HEAD_DIM)),
        'attn_k_gain': gain(ks[11], (N_ATTN_LAYERS, HEAD_DIM)),
        'attn_sink': nrm(ks[12], (N_ATTN_LAYERS, N_HEADS), 0.5),
        'hgrn_w_in': nrm(ks[13], (N_HGRN_LAYERS, D, HG_IN_W), D ** -0.5),
        'hgrn_w_out': nrm(ks[14], (N_HGRN_LAYERS, D, D), D ** -0.5),
        'hgrn_o_gain': gain(ks[15], (N_HGRN_LAYERS, HG_DV)),
        'hgrn_lb_logits': nrm(ks[16], (DEPTH, HG_DF), 0.5),
        'ffn_w_up': nrm(ks[17], (DEPTH, D, 2 * D_FF), D ** -0.5),
        'ffn_conv_w': nrm(ks[18], (DEPTH, CONV_W, D_FF), CONV_W ** -0.5),
        'ffn_conv_b': nrm(ks[19], (DEPTH, D_FF), 0.02),
        'ffn_w_down': nrm(ks[20], (DEPTH, D_FF, D), D_FF ** -0.5),
    }


def reference(x, c, ctx, c_ctx, ada_w, ada_b, norm1_g, norm2_g, attn_w_in, attn_w_out, attn_q_gain,
              attn_k_gain, attn_sink, hgrn_w_in, hgrn_w_out, hgrn_o_gain, hgrn_lb_logits, ffn_w_up,
              ffn_conv_w, ffn_conv_b, ffn_w_down):
    L = x.shape[1]
    rows = L // GRID_W
    ang_r, ang_c = axial_angles(rows)
    lb_prob = jax.nn.softmax(hgrn_lb_logits.astype(jnp.float32), axis=0)
    lb_sched = jnp.cumsum(lb_prob, axis=0) - lb_prob[0]
    for layer in range(DEPTH):
        last = layer == DEPTH - 1
        j = layer // N_MIXERS
        mod = jax.nn.silu(c) @ ada_w[layer] + ada_b[layer]
        mod_c = jax.nn.silu(c_ctx) @ ada_w[layer] + ada_b[layer]
        sh1, sc1, g1, sh2, sc2, g2 = jnp.split(mod[:, None, :], 6, axis=-1)
        csh1, csc1, cg1, csh2, csc2, cg2 = jnp.split(mod_c, 6)
        hx = rms_norm(x, norm1_g[layer]) * (1.0 + sc1) + sh1
        hc = rms_norm(ctx, norm1_g[layer]) * (1.0 + csc1) + csh1
        if layer % N_MIXERS == 0:
            yx, yc = windowed_gqa(hx, hc, attn_w_in[j], attn_w_out[j], attn_q_gain[j], attn_k_gain[j],
                                  attn_sink[j], ang_r, ang_c, not last)
        else:
            yx, yc = hgrn2_bidir(hx, hc, hgrn_w_in[j], hgrn_w_out[j], hgrn_o_gain[j], lb_sched[layer], not last)
        x = x + g1 * yx
        hx2 = rms_norm(x, norm2_g[layer]) * (1.0 + sc2) + sh2
        x = x + g2 * conv_glu(hx2, ffn_w_up[layer], ffn_conv_w[layer], ffn_conv_b[layer], ffn_w_down[layer])
        if not last:
            ctx = ctx + cg1 * yc
            hc2 = rms_norm(ctx, norm2_g[layer]) * (1.0 + csc2) + csh2
            ctx = ctx + cg2 * conv_glu(hc2, ffn_w_up[layer], ffn_conv_w[layer], ffn_conv_b[layer], ffn_w_down[layer])
    return x
```

```python
import math
from contextlib import ExitStack
import numpy as np
import concourse.bass as bass
import concourse.mybir as mybir
from concourse.bass_utils import run_bass_kernel_spmd

F32 = mybir.dt.float32
BF16 = mybir.dt.bfloat16
AF = mybir.ActivationFunctionType
ALU = mybir.AluOpType
AX = mybir.AxisListType

ENGS = ['sync', 'scalar', 'gpsimd', 'vector', 'tensor']
NDS = 8
D = 1024
KC = 8
DFF = 2816
NFC = 22
EPS = 1e-6
import os
CH_ENG = os.environ.get('CH_ENG', 'gpsimd')


class Res:
    __slots__ = ('name', 'w', 'r', 'hr')

    def __init__(self, name=''):
        self.name = name
        self.w = []
        self.r = {}
        self.hr = False


class Op:
    __slots__ = ('eng', 'fn', 'deps', 'need_inc', 'is_dma', 'sem', 'val')


class Sched:
    def __init__(self, nc, stack):
        self.nc = nc
        self.ops = {e: [] for e in ENGS}
        self.esem = {e: stack.enter_context(nc.semaphore('es_' + e)) for e in ENGS}
        self.dsem = {e: [stack.enter_context(nc.semaphore('ds_%s%d' % (e, i))) for i in range(NDS)]
                     for e in ENGS}
        self.dcnt = {e: 0 for e in ENGS}
        self.dlast = {e: {} for e in ENGS}
        self.ecnt = {e: 0 for e in ENGS}
        self.waited = {e: {} for e in ENGS}
        self.allres = []
        self.ninst = 0

    def res(self, name=''):
        r = Res(name)
        self.allres.append(r)
        return r

    def op(self, eng, fn, reads=(), writes=(), is_dma=False, accum=False):
        o = Op()
        o.eng = eng
        o.fn = fn
        o.need_inc = False
        o.is_dma = is_dma
        o.sem = None
        o.val = 0
        deps = {}
        for r in reads:
            for d in r.w:
                deps[id(d)] = d
        same_gen = {}
        for w in writes:
            sg = accum and (not w.hr) and len(w.w) > 0
            same_gen[id(w)] = sg
            if not sg:
                for d in w.w:
                    deps[id(d)] = d
            for d in w.r.values():
                if isinstance(d, list):
                    for dd in d:
                        deps[id(dd)] = dd
                else:
                    deps[id(d)] = d
        for r in reads:
            r.hr = True
            if is_dma:
                r.r.setdefault('dma', []).append(o)
            else:
                r.r[eng] = o
        for w in writes:
            if same_gen[id(w)]:
                w.w.append(o)
            else:
                w.w = [o]
                w.r = {}
                w.hr = False
        dl = []
        for d in deps.values():
            if d is o:
                continue
            if d.eng == eng and eng == 'tensor' and not d.is_dma and not is_dma:
                continue
            dl.append(d)
        o.deps = dl
        if is_dma:
            i = self.dcnt[eng]
            self.dcnt[eng] += 1
            o.sem = self.dsem[eng][i % NDS]
            o.val = 16 * (i // NDS + 1)
            prev = self.dlast[eng].get(i % NDS)
            if prev is not None and all(d is not prev for d in dl):
                dl.append(prev)
            self.dlast[eng][i % NDS] = o
        self.ops[eng].append(o)
        return o

    def dma(self, q, out, in_, reads=(), writes=(), accum=False, **kw):
        return self.op(q, lambda e: e.dma_start(out=out, in_=in_, **kw), reads, writes, is_dma=True, accum=accum)

    def mm(self, out, lhsT, rhs, start, stop, reads=(), writes=(), accum=False):
        return self.op('tensor', lambda e: e.matmul(out, lhsT=lhsT, rhs=rhs, start=start, stop=stop),
                       reads, writes, accum=accum)

    def tr(self, out, in_, ident, reads=(), writes=(), accum=False):
        return self.op('tensor', lambda e: e.transpose(out=out, in_=in_, identity=ident), reads, writes, accum=accum)

    def act(self, out, in_, func, reads=(), writes=(), accum=False, **kw):
        return self.op('scalar', lambda e: e.activation(out=out, in_=in_, func=func, **kw), reads, writes, accum=accum)

    def tt(self, eng, out, in0, in1, op, reads=(), writes=(), accum=False):
        return self.op(eng, lambda e: e.tensor_tensor(out=out, in0=in0, in1=in1, op=op), reads, writes, accum=accum)

    def ts(self, eng, out, in0, s1, s2, op0, op1=None, reads=(), writes=(), accum=False):
        if op1 is None:
            return self.op(eng, lambda e: e.tensor_scalar(out=out, in0=in0, scalar1=s1, scalar2=None, op0=op0),
                           reads, writes, accum=accum)
        return self.op(eng, lambda e: e.tensor_scalar(out=out, in0=in0, scalar1=s1, scalar2=s2, op0=op0, op1=op1),
                       reads, writes, accum=accum)

    def stt(self, eng, out, in0, scalar, in1, op0, op1, reads=(), writes=(), accum=False):
        return self.op(eng, lambda e: e.scalar_tensor_tensor(out=out, in0=in0, scalar=scalar, in1=in1,
                                                             op0=op0, op1=op1), reads, writes, accum=accum)

    def cp(self, eng, out, in_, reads=(), writes=(), accum=False):
        if eng == 'scalar':
            return self.op(eng, lambda e: e.copy(out=out, in_=in_), reads, writes, accum=accum)
        return self.op(eng, lambda e: e.tensor_copy(out=out, in_=in_), reads, writes, accum=accum)

    def memset(self, eng, ap, val, writes=(), accum=False):
        return self.op(eng, lambda e: e.memset(ap, val), (), writes, accum=accum)

    def red(self, out, in_, op, reads=(), writes=()):
        return self.op('vector', lambda e: e.tensor_reduce(out=out, in_=in_, axis=AX.X, op=op), reads, writes)

    def recip(self, out, in_, reads=(), writes=()):
        return self.op('vector', lambda e: e.reciprocal(out=out, in_=in_), reads, writes)

    def flush(self):
        for e in ENGS:
            for o in self.ops[e]:
                for d in o.deps:
                    if not d.is_dma:
                        d.need_inc = True
        for e in ENGS:
            lst = [o for o in self.ops[e] if not o.is_dma]
            if lst:
                lst[-1].need_inc = True
            c = self.ecnt[e]
            for o in self.ops[e]:
                if o.need_inc and not o.is_dma:
                    c += 1
                    o.sem = self.esem[e]
                    o.val = c
            self.ecnt[e] = c
        finals = []
        for e in ENGS:
            if self.ecnt[e] > 0:
                finals.append((self.esem[e], self.ecnt[e]))
            n = self.dcnt[e]
            for i in range(min(n, NDS)):
                cnt = (n - i + NDS - 1) // NDS
                finals.append((self.dsem[e][i], 16 * cnt))
        with self.nc.Block() as block:
            for e in ENGS:
                def body(engine, e=e):
                    waited = self.waited[e]
                    for o in self.ops[e]:
                        for d in o.deps:
                            if waited.get(d.sem.num, 0) >= d.val:
                                continue
                            engine.wait_ge(d.sem, d.val)
                            waited[d.sem.num] = d.val
                            self.ninst += 1
                        ins = o.fn(engine)
                        self.ninst += 1
                        if o.is_dma:
                            ins.then_inc(o.sem, 16)
                        elif o.need_inc:
                            ins.then_inc(o.sem, 1)
                    for sem, val in finals:
                        if waited.get(sem.num, 0) >= val:
                            continue
                        engine.wait_ge(sem, val)
                        waited[sem.num] = val
                getattr(block, e)(body)
        self.ops = {e: [] for e in ENGS}
        self.dlast = {e: {} for e in ENGS}
        for r in self.allres:
            r.w = []
            r.r = {}
            r.hr = False
        self.allres = []


class Ring:
    def __init__(self, items):
        self.items = items
        self.i = 0

    def next(self):
        it = self.items[self.i % len(self.items)]
        self.i += 1
        return it


class Cfg:
    def __init__(self, L=4096, CTX=256, depth=4, NS=2, do_mixer=True, do_ffn=True):
        self.L = L
        self.CTX = CTX
        self.depth = depth
        self.NS = NS
        self.NTC = CTX // 128
        self.NTX = L // 128
        self.NT = self.NTC + self.NTX
        self.TOK = CTX + L
        self.na = (depth + 1) // 2
        self.nh = depth // 2
        self.do_mixer = do_mixer
        self.do_ffn = do_ffn


def host_consts(cfg):
    c = {}
    i = np.arange(128)
    c['ident'] = np.eye(128, dtype=np.float32)
    s = i[:, None]
    t = i[None, :]
    c['mprev'] = (s >= t).astype(np.float32)
    c['mnext'] = (s <= t).astype(np.float32)
    same = ((s // 64) == (t // 64)).astype(np.float32)
    c['m1f'] = (s <= t).astype(np.float32) * same
    c['m2f'] = (s > t).astype(np.float32) * same
    c['m1b'] = (s >= t).astype(np.float32) * same
    c['m2b'] = (s < t).astype(np.float32) * same
    c['c2f'] = np.stack([(i < 64), (i >= 64)], 1).astype(np.float32)
    c['c2b'] = c['c2f'].copy()
    L = cfg.L
    pos = np.arange(L)
    row = (pos // 64).astype(np.float32)
    col = (pos % 64).astype(np.float32)
    inv = (10000.0 ** (-np.arange(16, dtype=np.float32) / 16)).astype(np.float32)
    ang = np.concatenate([row[:, None] * inv, col[:, None] * inv], 1).astype(np.float32)
    c['rcos'] = np.cos(ang).astype(np.float32)
    c['rsin'] = np.sin(ang).astype(np.float32)
    return c


CONST_SHAPES = lambda cfg: {
    'ident': [128, 128], 'mprev': [128, 128], 'mnext': [128, 128], 'm1f': [128, 128], 'm2f': [128, 128],
    'm1b': [128, 128], 'm2b': [128, 128], 'c2f': [128, 2], 'c2b': [128, 2],
    'rcos': [cfg.L, 32], 'rsin': [cfg.L, 32]}


def input_shapes(cfg):
    dp, na, nh = cfg.depth, cfg.na, cfg.nh
    return {
        'x': [cfg.NS, cfg.L, D], 'c': [cfg.NS, D], 'ctx': [cfg.NS, cfg.CTX, D], 'c_ctx': [D],
        'ada_w': [dp, D, 6 * D], 'ada_b': [dp, 6 * D], 'norm1_g': [dp, D], 'norm2_g': [dp, D],
        'attn_w_in': [na, D, 1536], 'attn_w_out': [na, D, D], 'attn_q_gain': [na, 64],
        'attn_k_gain': [na, 64], 'attn_sink': [na, 16],
        'hgrn_w_in': [max(nh, 1), D, 5120], 'hgrn_w_out': [max(nh, 1), D, D], 'hgrn_o_gain': [max(nh, 1), 128],
        'hgrn_lb_logits': [dp, 1024],
        'ffn_w_up': [dp, D, 2 * DFF], 'ffn_conv_w': [dp, 3, DFF], 'ffn_conv_b': [dp, DFF],
        'ffn_w_down': [dp, DFF, D]}


class Builder:
    def __init__(self, cfg):
        self.cfg = cfg
        nc = bass.Bass("TRN2", target_bir_lowering=False)
        self.nc = nc
        self.I = {k: nc.dram_tensor(k, shp, F32, kind="ExternalInput").ap() for k, shp in input_shapes(cfg).items()}
        self.C = {k: nc.dram_tensor('k_' + k, shp, F32, kind="ExternalInput").ap()
                  for k, shp in CONST_SHAPES(cfg).items()}
        self.out = nc.dram_tensor("out", [cfg.NS, cfg.L, D], F32, kind="ExternalOutput").ap()
        self.xs = nc.dram_tensor("xs", [cfg.NS, cfg.TOK, D], F32, kind="Internal").ap()
        self.modv = nc.dram_tensor("modv", [cfg.depth, 3, 6 * D], F32, kind="Internal").ap()
        self.written = set()
        if cfg.nh > 0:
            NS, NT = cfg.NS, cfg.NT
            dkind = "ExternalOutput" if getattr(cfg, 'debug', False) else "Internal"
            dt = lambda n, shp, d: nc.dram_tensor(n, shp, d, kind=dkind).ap()
            self.lbv = dt("lbv", [cfg.nh, 2, D], F32)
            self.hqb = dt("hqb", [NS, NT, 128, KC, 128], BF16)
            self.hkb = dt("hkb", [NS, NT, 128, KC, 128], BF16)
            self.hkd = dt("hkd", [NS, NT, 128, D], BF16)
            self.hv = dt("hv", [NS, NT, 128, D], BF16)
            self.hsg = dt("hsg", [NS, NT, 128, D], BF16)
            self.hdb = dt("hdb", [NS, NT, 128, 16], F32)
            self.hof = dt("hof", [NS, NT, 128, D], F32)

    def x_src(self, s, t):
        cfg = self.cfg
        if (s, t) in self.written:
            return self.xs[s, t * 128:(t + 1) * 128, :]
        if t < cfg.NTC:
            return self.I['ctx'][s, t * 128:(t + 1) * 128, :]
        tt = t - cfg.NTC
        return self.I['x'][s, tt * 128:(tt + 1) * 128, :]

    def x_dst(self, s, t, final=False):
        cfg = self.cfg
        if final:
            tt = t - cfg.NTC
            return self.out[s, tt * 128:(tt + 1) * 128, :]
        return self.xs[s, t * 128:(t + 1) * 128, :]

    def build(self):
        cfg = self.cfg
        nc = self.nc
        with ExitStack() as top:
            self.S = Sched(nc, top)
            self.phase_mod()
            if cfg.nh > 0 and cfg.do_mixer:
                self.phase_lb()
            for layer in range(cfg.depth):
                last = layer == cfg.depth - 1
                if cfg.do_mixer:
                    if layer % 2 == 0:
                        self.phase_attn(layer, layer // 2, last)
                    else:
                        self.phase_hgrn(layer, layer // 2, last)
                if cfg.do_ffn:
                    self.phase_ffn(layer, last)
        return nc

    def alloc(self, st, name, shape, dt):
        self.uid = getattr(self, 'uid', 0) + 1
        return st.enter_context(self.nc.sbuf_tensor('%s_%d' % (name, self.uid), shape, dt))

    def palloc(self, st, name, shape, dt):
        self.uid = getattr(self, 'uid', 0) + 1
        return st.enter_context(self.nc.psum_tensor('%s_%d' % (name, self.uid), shape, dt))

    def ring(self, st, name, n, shape, dt, psum=False):
        S = self.S
        items = []
        for i in range(n):
            t = (self.palloc if psum else self.alloc)(st, '%s%d' % (name, i), shape, dt)
            items.append((t, S.res('%s%d' % (name, i))))
        return Ring(items)

    def load_const(self, st, key, dt, q='sync'):
        S = self.S
        shp = list(self.C[key].shape)
        t = self.alloc(st, 'c_' + key, shp, dt)
        r = S.res(key)
        S.dma('gpsimd' if dt != F32 else q, t[:], self.C[key], writes=[r])
        return t, r

    def load_mod(self, st, layer, rset, sub, name, norm_g, need='GSA'):
        S = self.S
        G = self.alloc(st, name + 'G', [128, D], F32) if 'G' in need else None
        SH = self.alloc(st, name + 'SH', [128, D], F32) if 'S' in need else None
        GA = self.alloc(st, name + 'GA', [128, D], F32) if 'A' in need else None
        r = S.res(name)
        return (G, SH, GA, r)

    def fill_mod(self, mod, layer, rset, sub, norm_g, tmp, tmpr):
        S = self.S
        G, SH, GA, r = mod
        base = 3 * sub * D
        mv = self.modv
        if SH is not None:
            S.dma('sync', SH[:], mv[layer, rset, base:base + D].partition_broadcast(128), writes=[r], accum=True)
        if GA is not None:
            S.dma('sync', GA[:], mv[layer, rset, base + 2 * D:base + 3 * D].partition_broadcast(128), writes=[r], accum=True)
        if G is not None:
            S.dma('sync', G[:], mv[layer, rset, base + D:base + 2 * D].partition_broadcast(128), writes=[r], accum=True)
            S.dma('sync', tmp[:], norm_g[layer].partition_broadcast(128), writes=[tmpr])
            S.stt('vector', G[:], G[:], 1.0, tmp[:], ALU.add, ALU.mult, reads=[r, tmpr], writes=[r])

    def norm_tile(self, xt, xr, mod, stat, statr, hb, hbr):
        S = self.S
        G, SH, GA, mr = mod
        S.act(hb[:], xt[:], AF.Square, reads=[xr], writes=[hbr, statr], accum_out=stat[:, 0:1])
        S.act(stat[:, 1:2], stat[:, 0:1], AF.Ln, reads=[statr, self.epsr], writes=[statr], scale=1.0 / D, bias=self.epsb[:, 0:1])
        S.act(stat[:, 2:3], stat[:, 1:2], AF.Exp, reads=[statr], writes=[statr], scale=-0.5)
        S.stt('vector', xt[:], xt[:], stat[:, 2:3], G[:], ALU.mult, ALU.mult, reads=[xr, statr, mr], writes=[xr])
        S.tt('vector', hb[:], xt[:], SH[:], ALU.add, reads=[xr, mr], writes=[hbr])

    def common_consts(self, st):
        S = self.S
        self.ident, self.identr = self.load_const(st, 'ident', BF16)
        self.epsb = self.alloc(st, 'epsb', [128, 1], F32)
        self.epsr = S.res('epsb')
        S.memset('vector', self.epsb[:], EPS, writes=[self.epsr])

    def phase_mod(self):
        cfg, S, nc = self.cfg, self.S, self.nc
        with ExitStack() as st:
            cT = self.alloc(st, 'cT', [128, 3, 8], F32)
            cTr = S.res('cT')
            scb = self.alloc(st, 'scb', [128, 3, 8], F32)
            scbr = S.res('scb')
            for r in range(cfg.NS):
                S.dma('sync', cT[:, r, :], self.I['c'][r].rearrange("(p kc) -> p kc", kc=8), writes=[cTr], accum=True)
            if cfg.NS < 2:
                S.dma('sync', cT[:, 1, :], self.I['c'][0].rearrange("(p kc) -> p kc", kc=8), writes=[cTr], accum=True)
            S.dma('sync', cT[:, 2, :], self.I['c_ctx'].rearrange("(p kc) -> p kc", kc=8), writes=[cTr], accum=True)
            S.act(scb[:], cT[:], AF.Silu, reads=[cTr], writes=[scbr])
            wring = self.ring(st, 'adaw', 4, [128, 8, 512], F32)
            bring = self.ring(st, 'adab', 2, [3, 6 * D], F32)
            pring = self.ring(st, 'pmod', 2, [3, 512], F32, psum=True)
            mring = self.ring(st, 'mrow', 2, [3, 512], F32)
            qi = 0
            for layer in range(cfg.depth):
                bt, br = bring.next()
                S.dma('sync', bt[:], self.I['ada_b'][layer].partition_broadcast(3), writes=[br])
                wv = self.I['ada_w'][layer].rearrange("(p kc) n -> p kc n", kc=8)
                for nb in range(12):
                    wt, wr = wring.next()
                    S.dma(('sync', 'scalar')[qi % 2], wt[:], wv[:, :, nb * 512:(nb + 1) * 512], writes=[wr])
                    qi += 1
                    pt, pr = pring.next()
                    for kc in range(8):
                        S.mm(pt[:], scb[:, :, kc], wt[:, kc, :], kc == 0, kc == 7, reads=[scbr, wr], writes=[pr])
                    mt, mr = mring.next()
                    S.tt('vector', mt[:], pt[:], bt[:, nb * 512:(nb + 1) * 512], ALU.add, reads=[pr, br], writes=[mr])
                    S.dma('sync', self.modv[layer, :, nb * 512:(nb + 1) * 512], mt[:], reads=[mr])
            S.flush()

    def phase_ffn(self, layer, last):
        cfg, S, nc = self.cfg, self.S, self.nc
        with ExitStack() as st:
            self.common_consts(st)
            wup = self.alloc(st, 'wup', [128, KC, 2 * DFF], BF16)
            wupr = S.res('wup')
            wdn = self.alloc(st, 'wdn', [128, NFC, D], BF16)
            wdnr = S.res('wdn')
            wu = self.I['ffn_w_up'][layer].rearrange("(kc p) n -> p kc n", p=128)
            for kc in range(KC):
                S.dma('gpsimd', wup[:, kc, :], wu[:, kc, :], writes=[wupr], accum=True)
            wd = self.I['ffn_w_down'][layer].rearrange("(c p) n -> p c n", p=128)
            for c0 in range(0, NFC, 6):
                c1 = min(NFC, c0 + 6)
                S.dma('gpsimd', wdn[:, c0:c1, :], wd[:, c0:c1, :], writes=[wdnr], accum=True)
            cw = self.alloc(st, 'cw', [128, 3, NFC], F32)
            cb = self.alloc(st, 'cb', [128, NFC], F32)
            cwr = S.res('cw')
            for j in range(3):
                S.dma('sync', cw[:, j, :], self.I['ffn_conv_w'][layer, j].rearrange("(c p) -> p c", p=128),
                      writes=[cwr], accum=True, allow_slow_non_contiguous=True)
            S.dma('sync', cb[:], self.I['ffn_conv_b'][layer].rearrange("(c p) -> p c", p=128), writes=[cwr], accum=True,
                  allow_slow_non_contiguous=True)
            mod = self.load_mod(st, layer, 0, 1, 'fm', None)
            xring = self.ring(st, 'fx', 2, [128, D], F32)
            xdring = self.ring(st, 'fxd', 1, [128, D], F32)
            pre_x = {}
            hbring = self.ring(st, 'fhb', 2, [128, D], BF16)
            string = self.ring(st, 'fst', 4, [128, 4], F32)
            hT = [self.alloc(st, 'fhT%d' % i, [128, KC, 514], BF16) for i in range(2)]
            hTr = [[S.res('hT%d_%d' % (i, j)) for j in range(4)] for i in range(2)]
            hTL = [S.res('hTL%d' % i) for i in range(2)]
            hTR = [S.res('hTR%d' % i) for i in range(2)]
            actT = self.alloc(st, 'actT', [128, NFC, 512], BF16)
            actr = [S.res('act%d' % c) for c in range(NFC)]
            gbring = self.ring(st, 'gb', 2, [128, 514], F32)
            t1ring = self.ring(st, 't1', 2, [128, 512], F32)
            ptr = self.ring(st, 'ptr', 1, [128, KC, 128], BF16, psum=True)
            pg = self.ring(st, 'pg', 2, [128, 512], F32, psum=True)
            pv = self.ring(st, 'pv', 2, [128, 512], F32, psum=True)
            ph = self.ring(st, 'ph', 1, [128, 512], F32, psum=True)
            py = self.ring(st, 'py', 2, [128, 512], F32, psum=True)
            phi = [0]

            def prefetch(s, t):
                xt, xr = xring.next()
                S.dma('sync', xt[:], self.x_src(s, t), writes=[xr])
                pre_x[(s, t)] = (xt, xr)

            def prepA(s, t):
                if (s, t) not in pre_x:
                    prefetch(s, t)
                xt, xr = pre_x.pop((s, t))
                hb, hbr = hbring.next()
                stt_, str_ = string.next()
                self.norm_tile(xt, xr, mod, stt_, str_, hb, hbr)
                return hb, hbr

            def prepB(hb, hbr, buf, slot, left_to, right_to):
                pt, pr = ptr.next()
                for kc in range(KC):
                    S.tr(pt[:, kc, :], hb[:, kc * 128:(kc + 1) * 128], self.ident[:], reads=[hbr, self.identr],
                         writes=[pr])
                S.cp('scalar', hT[buf][:, :, 1 + slot * 128:1 + (slot + 1) * 128], pt[:], reads=[pr],
                     writes=[hTr[buf][slot]])
                if left_to is not None:
                    S.cp('scalar', hT[left_to][:, :, 0:1], pt[:, :, 127:128], reads=[pr], writes=[hTL[left_to]])
                if right_to is not None:
                    b, col = right_to
                    S.cp('scalar', hT[b][:, :, col:col + 1], pt[:, :, 0:1], reads=[pr], writes=[hTR[b]])

            def up(buf, nt):
                n = nt * 128
                for c in range(NFC):
                    g, gr = pg.next()
                    v, vr = pv.next()
                    h, hr = ph.next()
                    hc = (phi[0] % 16) * 2
                    phi[0] += 1
                    rd = [wupr] + hTr[buf][:nt]
                    for kc in range(KC):
                        S.mm(g[:, 0:n], wup[:, kc, c * 128:(c + 1) * 128], hT[buf][:, kc, 1:1 + n], kc == 0, kc == KC - 1,
                             reads=rd, writes=[gr])
                    for kc in range(KC):
                        S.mm(h[:, hc:hc + 2], wup[:, kc, c * 128:(c + 1) * 128], hT[buf][:, kc, 0:n + 2:n + 1],
                             kc == 0, kc == KC - 1, reads=[wupr, hTL[buf], hTR[buf]], writes=[hr])
                    for kc in range(KC):
                        S.mm(v[:, 0:n], wup[:, kc, DFF + c * 128:DFF + (c + 1) * 128], hT[buf][:, kc, 1:1 + n],
                             kc == 0, kc == KC - 1, reads=rd, writes=[vr])
                    gb, gbr = gbring.next()
                    S.cp('scalar', gb[:, 1:1 + n], g[:, 0:n], reads=[gr], writes=[gbr])
                    S.cp('scalar', gb[:, 0:n + 2:n + 1], h[:, hc:hc + 2], reads=[hr], writes=[gbr])
                    t1, t1r = t1ring.next()
                    S.ts('vector', t1[:, 0:n], gb[:, 1:1 + n], cw[:, 1, c:c + 1], cb[:, c:c + 1], ALU.mult, ALU.add,
                         reads=[gbr, cwr], writes=[t1r])
                    S.stt('vector', t1[:, 0:n], gb[:, 0:n], cw[:, 0, c:c + 1], t1[:, 0:n], ALU.mult, ALU.add,
                          reads=[gbr, cwr, t1r], writes=[t1r])
                    S.stt('vector', t1[:, 0:n], gb[:, 2:2 + n], cw[:, 2, c:c + 1], t1[:, 0:n], ALU.mult, ALU.add,
                          reads=[gbr, cwr, t1r], writes=[t1r])
                    S.act(t1[:, 0:n], t1[:, 0:n], AF.Silu, reads=[t1r], writes=[t1r])
                    S.tt('vector', actT[:, c, 0:n], t1[:, 0:n], v[:, 0:n], ALU.mult, reads=[t1r, vr], writes=[actr[c]])
                    yield

            def down(s, tiles, final):
                G, SH, GA, mr = mod
                for j, t in enumerate(tiles):
                    xt, xr = xdring.next()
                    S.dma('sync', xt[:], self.x_src(s, t), writes=[xr])
                    for half in range(2):
                        y, yr = py.next()
                        for c in range(NFC):
                            S.mm(y[:], actT[:, c, j * 128:(j + 1) * 128], wdn[:, c, half * 512:(half + 1) * 512],
                                 c == 0, c == NFC - 1, reads=[actr[c], wdnr], writes=[yr])
                        tmp, tmpr = t1ring.next()
                        S.tt('vector', tmp[:], y[:], GA[:, half * 512:(half + 1) * 512], ALU.mult, reads=[yr, mr],
                             writes=[tmpr])
                        S.tt('vector', xt[:, half * 512:(half + 1) * 512], tmp[:], xt[:, half * 512:(half + 1) * 512],
                             ALU.add, reads=[tmpr, xr], writes=[xr])
                    S.dma('gpsimd', self.x_dst(s, t, final), xt[:], reads=[xr])
                    self.written.add((s, t))
                    yield

            bufi = [0]
            for s in range(cfg.NS):
                segs = []
                if not last:
                    segs.append((2, list(range(cfg.NTC))))
                segs.append((s, list(range(cfg.NTC, cfg.NT))))
                for rset, tiles in segs:
                    tmpt, tmpr = xdring.next()
                    self.fill_mod(mod, layer, rset, 1, self.I['norm2_g'], tmpt, tmpr)
                    blocks = [tiles[i:i + 4] for i in range(0, len(tiles), 4)]
                    nb = len(blocks)
                    bufs = [(bufi[0] + m) % 2 for m in range(nb)]
                    bufi[0] += nb

                    order = [(0, j) for j in range(len(blocks[0]))]
                    for m_ in range(nb):
                        if m_ + 1 < nb:
                            n1_ = len(blocks[m_ + 1])
                            if m_ == 0:
                                order.append((1, 0))
                            order += [(m_ + 1, j) for j in range(1, n1_ - 1)]
                            if n1_ > 1:
                                order.append((m_ + 1, n1_ - 1))
                            if m_ + 2 < nb:
                                order.append((m_ + 2, 0))
                    pos = {mj: i for i, mj in enumerate(order)}

                    def prep_tile(m, j):
                        blk = blocks[m]
                        left_to = bufs[m + 1] if (j == len(blk) - 1 and m + 1 < nb) else None
                        right_to = (bufs[m - 1], 1 + len(blocks[m - 1]) * 128) if (j == 0 and m > 0) else None
                        hb, hbr = prepA(s, blk[j])
                        i = pos[(m, j)]
                        if i + 1 < len(order):
                            m2, j2 = order[i + 1]
                            prefetch(s, blocks[m2][j2])
                        yield
                        prepB(hb, hbr, bufs[m], j, left_to, right_to)
                        yield

                    def preps(lst):
                        for (m_, j_) in lst:
                            yield from prep_tile(m_, j_)

                    def run(g):
                        for _ in g:
                            pass

                    S.memset('vector', hT[bufs[0]][:, :, 0:1], 0.0, writes=[hTL[bufs[0]]])
                    for j in range(len(blocks[0])):
                        run(prep_tile(0, j))
                    if nb > 1:
                        run(prep_tile(1, 0))
                    for m in range(nb):
                        nt = len(blocks[m])
                        if m + 1 >= nb:
                            S.memset('vector', hT[bufs[m]][:, :, 1 + nt * 128:2 + nt * 128], 0.0, writes=[hTR[bufs[m]]])
                        gens = [up(bufs[m], nt)]
                        if m + 1 < nb:
                            n1 = len(blocks[m + 1])
                            gens.append(preps([(m + 1, j) for j in range(1, n1 - 1)]))
                        for _ in ileave(gens, [3, 1]):
                            pass
                        gens = [down(s, blocks[m], last)]
                        later = []
                        if m + 1 < nb and len(blocks[m + 1]) > 1:
                            later.append((m + 1, len(blocks[m + 1]) - 1))
                        if m + 2 < nb:
                            later.append((m + 2, 0))
                        if later:
                            gens.append(preps(later))
                        for _ in ileave(gens, [1, 1]):
                            pass
            S.flush()

    def phase_attn(self, layer, j, last):
        cfg, S, nc = self.cfg, self.S, self.nc
        NTC, NTX, NT = cfg.NTC, cfg.NTX, cfg.NT
        with ExitStack() as st:
            self.common_consts(st)
            mprev, mprevr = self.load_const(st, 'mprev', BF16)
            mnext, mnextr = self.load_const(st, 'mnext', BF16)
            win = self.alloc(st, 'win', [128, KC, 1536], BF16)
            winr = S.res('win')
            wsrc = self.I['attn_w_in'][j]
            qk_src = wsrc[:, 0:1280].rearrange("(kc p) (h a b i) -> p kc h a b i", p=128, a=2, b=2, i=16)
            qk_dst = win[:, :, 0:1280].rearrange("p kc (h b a i) -> p kc h b a i", b=2, a=2, i=16)
            v_src = wsrc[:, 1280:1536].rearrange("(kc p) n -> p kc n", p=128)
            for kc in range(KC):
                for a in range(2):
                    for b in range(2):
                        S.dma('gpsimd', qk_dst[:, kc, :, b, a, :], qk_src[:, kc, :, a, b, :], writes=[winr], accum=True)
            S.dma('gpsimd', win[:, :, 1280:1536], v_src, writes=[winr], accum=True)
            wout = self.alloc(st, 'wout', [128, KC, D], BF16)
            woutr = S.res('wout')
            S.dma('gpsimd', wout[:], self.I['attn_w_out'][j].rearrange("(kc p) n -> p kc n", p=128), writes=[woutr])
            g64 = self.alloc(st, 'g64', [128, 2, 64], F32)
            g64r = S.res('g64')
            for qi, key in enumerate(('attn_q_gain', 'attn_k_gain')):
                gsrc = self.I[key][j].rearrange("(a b i) -> a b i", a=2, b=2)
                gdst = g64[:, qi, :].rearrange("p (b a i) -> p b a i", b=2, a=2)
                for a in range(2):
                    for b in range(2):
                        S.dma('sync', gdst[:, b, a, :], gsrc[a, b, :].partition_broadcast(128), writes=[g64r], accum=True)
            GN = self.alloc(st, 'GN', [128, 1280], F32)
            GNr = S.res('GN')
            S.ts('vector', GN[:, 0:1024].rearrange("p (h f) -> p h f", f=64),
                 g64[:, 0, :].unsqueeze(1).to_broadcast([128, 16, 64]), 0.125, None, ALU.mult, reads=[g64r], writes=[GNr])
            S.cp('vector', GN[:, 1024:1280].rearrange("p (h f) -> p h f", f=64),
                 g64[:, 1, :].unsqueeze(1).to_broadcast([128, 4, 64]), reads=[g64r], writes=[GNr])
            esink = self.alloc(st, 'esink', [128, 16], F32)
            esr = S.res('esink')
            S.dma('sync', esink[:], self.I['attn_sink'][j].partition_broadcast(128), writes=[esr])
            S.act(esink[:], esink[:], AF.Exp, reads=[esr], writes=[esr])
            cosT = self.alloc(st, 'cosT', [128, NTX, 32], F32)
            sinT = self.alloc(st, 'sinT', [128, NTX, 32], F32)
            ropr = S.res('rope')
            S.dma('sync', cosT[:], self.C['rcos'].rearrange("(t p) f -> p t f", p=128), writes=[ropr], accum=True)
            S.dma('sync', sinT[:], self.C['rsin'].rearrange("(t p) f -> p t f", p=128), writes=[ropr], accum=True)
            modC = self.load_mod(st, layer, 2, 0, 'amC', None)
            modS = self.load_mod(st, layer, 0, 0, 'amS', None)
            KT = self.alloc(st, 'KT', [128, 2, cfg.TOK], BF16)
            KTr = [S.res('KT%d' % t) for t in range(NT)]
            VA = self.alloc(st, 'VA', [128, NT, 4, 66], BF16)
            VAr = [S.res('VA%d' % t) for t in range(NT)]
            VA1 = S.res('VAones')
            xring = self.ring(st, 'ax', 2, [128, D], F32)
            xaring = self.ring(st, 'axa', 1, [128, D], F32)
            pre_x = {}
            hbring = self.ring(st, 'ahb', 2, [128, D], BF16)
            string = self.ring(st, 'ast', 4, [128, 4], F32)
            hTring = self.ring(st, 'ahT', 2, [128, KC, 128], BF16)
            sqring = self.ring(st, 'asq', 1, [128, 1280], F32)
            qnring = self.ring(st, 'aqn', 1, [128, 1280], F32)
            Bring = self.ring(st, 'aB', 1, [128, 1280], F32)
            qkbring = self.ring(st, 'aqkb', 2, [128, 1280], BF16)
            ssring = self.ring(st, 'ass', 2, [128, 40], F32)
            QTring = self.ring(st, 'aQT', 4, [128, 8, 128], BF16)
            PTring = self.ring(st, 'aPT', 4, [128, 5, 512], BF16)
            obring = self.ring(st, 'aob', 2, [128, D], BF16)
            oTring = self.ring(st, 'aoT', 1, [128, KC, 128], BF16)
            dnring = self.ring(st, 'adn', 4, [128, 8], F32)
            tmpring = self.ring(st, 'atmp', 1, [128, 512], F32)
            ptr = self.ring(st, 'aptr', 1, [128, KC, 128], BF16, psum=True)
            pqkv = self.ring(st, 'apq', 1, [128, 1536], F32, psum=True)
            pO = self.ring(st, 'apO', 2, [128, 4, 66], F32, psum=True)
            qt_of = {}

            pSY = self.alloc_psum_pair(st)

            def prefetch(s, t):
                xt, xr = xring.next()
                S.dma('sync', xt[:], self.x_src(s, t), writes=[xr])
                pre_x[(s, t)] = (xt, xr)

            normed = {}

            def do_norm(s, t):
                mod = modC if t < NTC else modS
                if (s, t) not in pre_x:
                    prefetch(s, t)
                xt, xr = pre_x.pop((s, t))
                hb, hbr = hbring.next()
                stt_, str_ = string.next()
                self.norm_tile(xt, xr, mod, stt_, str_, hb, hbr)
                normed[(s, t)] = (hb, hbr)
                if t + 1 < NT:
                    prefetch(s, t + 1)

            def proj(s, t):
                if (s, t) not in normed:
                    do_norm(s, t)
                hb, hbr = normed.pop((s, t))
                yield
                pt, pr = ptr.next()
                for kc in range(KC):
                    S.tr(pt[:, kc, :], hb[:, kc * 128:(kc + 1) * 128], self.ident[:], reads=[hbr, self.identr], writes=[pr])
                hT, hTr = hTring.next()
                S.cp('scalar', hT[:], pt[:], reads=[pr], writes=[hTr])
                yield
                pq, pqr = pqkv.next()
                for nb in range(3):
                    for kc in range(KC):
                        S.mm(pq[:, nb * 512:(nb + 1) * 512], hT[:, kc, :], win[:, kc, nb * 512:(nb + 1) * 512],
                             kc == 0, kc == KC - 1, reads=[hTr, winr], writes=[pqr])
                    yield
                sq, sqr = sqring.next()
                S.act(sq[:], pq[:, 0:1280], AF.Square, reads=[pqr], writes=[sqr])
                S.cp('scalar', VA[:, t, :, 0:64], pq[:, 1280:1536].rearrange("p (h f) -> p h f", f=64), reads=[pqr],
                     writes=[VAr[t]])
                yield
                ss, ssr = ssring.next()
                S.red(ss[:, 0:20], sq[:].rearrange("p (h f) -> p h f", f=64), ALU.add, reads=[sqr], writes=[ssr])
                S.act(ss[:, 20:40], ss[:, 0:20], AF.Ln, reads=[ssr, self.epsr], writes=[ssr], scale=1.0 / 64, bias=self.epsb[:, 0:1])
                S.act(ss[:, 20:40], ss[:, 20:40], AF.Exp, reads=[ssr], writes=[ssr], scale=-0.5)
                yield
                qn, qnr = qnring.next()
                S.tt('vector', qn[:].rearrange("p (h f) -> p h f", f=64), pq[:, 0:1280].rearrange("p (h f) -> p h f", f=64),
                     ss[:, 20:40].unsqueeze(2).to_broadcast([128, 20, 64]), ALU.mult, reads=[pqr, ssr], writes=[qnr])
                yield
                S.tt('vector', qn[:], qn[:], GN[:], ALU.mult, reads=[qnr, GNr], writes=[qnr])
                yield
                qkb, qkbr = qkbring.next()
                qo = qkb[:, 0:1024].rearrange("p (h a b f) -> p a h b f", a=2, b=2, f=32)
                ko = qkb[:, 1024:1280].rearrange("p (k a b f) -> p a k b f", a=2, b=2, f=32)
                v4 = lambda ap: ap.rearrange("p (h b f) -> p h b f", b=2, f=32)
                vq = lambda ap: v4(ap)[:, 0:16].rearrange("p (a h) b f -> p a h b f", a=2)
                vk = lambda ap: v4(ap)[:, 16:20].rearrange("p (a k) b f -> p a k b f", a=2)
                if t >= NTC:
                    tt_ = t - NTC
                    Bt, Br = Bring.next()
                    cb_ = cosT[:, tt_, :].unsqueeze(1).unsqueeze(1).to_broadcast([128, 20, 2, 32])
                    sb_ = sinT[:, tt_, :].unsqueeze(1).unsqueeze(1).to_broadcast([128, 20, 2, 32])
                    S.tt('vector', v4(sq[:]), v4(qn[:]), cb_, ALU.mult, reads=[qnr, ropr], writes=[sqr])
                    S.tt('vector', v4(Bt[:]), v4(qn[:]), sb_, ALU.mult, reads=[qnr, ropr], writes=[Br])
                    yield
                    for vv, oo, eng_ in ((vq, qo, 'vector'), (vk, ko, 'gpsimd')):
                        S.tt(eng_, oo[:, :, :, 0, :], vv(sq[:])[:, :, :, 0, :], vv(Bt[:])[:, :, :, 1, :], ALU.subtract,
                             reads=[sqr, Br], writes=[qkbr], accum=True)
                        S.tt(eng_, oo[:, :, :, 1, :], vv(Bt[:])[:, :, :, 0, :], vv(sq[:])[:, :, :, 1, :], ALU.add,
                             reads=[sqr, Br], writes=[qkbr], accum=True)
                    yield
                else:
                    S.cp('gpsimd', qo, vq(qn[:]), reads=[qnr], writes=[qkbr], accum=True)
                    S.cp('gpsimd', ko, vk(qn[:]), reads=[qnr], writes=[qkbr], accum=True)
                    yield
                if t + 1 < NT:
                    do_norm(s, t + 1)
                    yield
                QT, QTr = QTring.next()
                qt_of[(s, t)] = (QT, QTr)
                pt, pr = ptr.next()
                for h in range(8):
                    S.tr(pt[:, h, :], qkb[:, h * 128:(h + 1) * 128], self.ident[:], reads=[qkbr, self.identr], writes=[pr])
                S.cp('vector', QT[:], pt[:], reads=[pr], writes=[QTr])
                yield
                pt, pr = ptr.next()
                for k_ in range(2):
                    S.tr(pt[:, k_, :], qkb[:, 1024 + k_ * 128:1024 + (k_ + 1) * 128], self.ident[:],
                         reads=[qkbr, self.identr], writes=[pr])
                S.cp('scalar', KT[:, :, t * 128:(t + 1) * 128], pt[:, 0:2, :], reads=[pr], writes=[KTr[t]])
                yield

            def attend_kv(kv, chunks, QT, QTr, ob, obr):
                nch = len(chunks)
                PT, PTr = PTring.next()
                for ci, (kt, mk, mkr) in enumerate(chunks):
                    sp, spr = pSY[(ci + (kv // 2)) % 2]
                    ph = slice(0, 64) if kv < 2 else slice(64, 128)
                    kk_ = kv % 2
                    S.mm(sp[:], KT[ph, kk_, kt * 128:(kt + 1) * 128], QT[ph, kk_ * 4:(kk_ + 1) * 4, :], True, True,
                         reads=[KTr[kt], QTr], writes=[spr])
                    S.act(PT[:, ci, :], sp[:], AF.Exp, reads=[spr], writes=[PTr], accum=True)
                    if mk is not None:
                        pv_ = PT[:, ci, :].rearrange("p (g q) -> p g q", g=4)
                        S.tt('gpsimd', pv_, pv_, mk[:].unsqueeze(1).to_broadcast([128, 4, 128]), ALU.mult,
                             reads=[PTr, mkr], writes=[PTr])
                    yield
                O, Or = pO.next()
                for g in range(4):
                    for ci, (kt, mk, mkr) in enumerate(chunks):
                        S.mm(O[:, g, 0:65], PT[:, ci, g * 128:(g + 1) * 128], VA[:, kt, kv, 0:65], ci == 0, ci == nch - 1,
                             reads=[PTr, VAr[kt], VA1], writes=[Or])
                dn, dnr = dnring.next()
                S.tt('vector', dn[:, 0:4], O[:, :, 64], esink[:, kv * 4:(kv + 1) * 4], ALU.add, reads=[Or, esr], writes=[dnr])
                S.recip(dn[:, 4:8], dn[:, 0:4], reads=[dnr], writes=[dnr])
                S.tt('vector', ob[:, kv * 256:(kv + 1) * 256].rearrange("p (g f) -> p g f", f=64), O[:, :, 0:64],
                     dn[:, 4:8].unsqueeze(2).to_broadcast([128, 4, 64]), ALU.mult, reads=[Or, dnr], writes=[obr], accum=True)
                yield

            def attend(s, t):
                mod = modC if t < NTC else modS
                QT, QTr = qt_of.pop((s, t))
                if t < NTC:
                    chunks = [(k, None, None) for k in range(NTC)]
                else:
                    chunks = []
                    if t > NTC:
                        chunks.append((t - 1, mprev, mprevr))
                    chunks.append((t, None, None))
                    if t < NT - 1:
                        chunks.append((t + 1, mnext, mnextr))
                    chunks += [(k, None, None) for k in range(NTC)]
                ob, obr = obring.next()
                yield from ileave([attend_kv(kv_, chunks, QT, QTr, ob, obr) for kv_ in (0, 2, 1, 3)])
                pt, pr = ptr.next()
                for kc in range(KC):
                    S.tr(pt[:, kc, :], ob[:, kc * 128:(kc + 1) * 128], self.ident[:], reads=[obr, self.identr], writes=[pr])
                oT, oTr = oTring.next()
                S.cp('scalar', oT[:], pt[:], reads=[pr], writes=[oTr])
                yield
                xt, xr = xaring.next()
                S.dma('sync', xt[:], self.x_src(s, t), writes=[xr])
                G, SH, GA, mr = mod
                for half in range(2):
                    y, yr = pSY[half]
                    for kc in range(KC):
                        S.mm(y[:], oT[:, kc, :], wout[:, kc, half * 512:(half + 1) * 512],
                             kc == 0, kc == KC - 1, reads=[oTr, woutr], writes=[yr])
                    tmp, tmpr = tmpring.next()
                    S.tt('vector', tmp[:], y[:], GA[:, half * 512:(half + 1) * 512], ALU.mult,
                         reads=[yr, mr], writes=[tmpr])
                    S.tt('gpsimd', xt[:, half * 512:(half + 1) * 512], tmp[:], xt[:, half * 512:(half + 1) * 512],
                         ALU.add, reads=[tmpr, xr], writes=[xr])
                S.dma('sync', self.x_dst(s, t), xt[:], reads=[xr])
                self.written.add((s, t))
                yield

            S.memset('vector', VA[:, :, :, 64:66], 1.0, writes=[VA1])
            tmpt, tmpr_ = xaring.next()
            self.fill_mod(modC, layer, 2, 0, self.I['norm1_g'], tmpt, tmpr_)
            for s in range(cfg.NS):
                tmpt, tmpr_ = xaring.next()
                self.fill_mod(modS, layer, s, 0, self.I['norm1_g'], tmpt, tmpr_)
                pend = []
                if not last:
                    pend += [(t, NTC - 1) for t in range(NTC)]
                pend += [(t, min(t + 1, NT - 1)) for t in range(NTC, NT)]
                for k in range(NT):
                    gens = [proj(s, k)]
                    if pend and pend[0][1] < k:
                        gens.append(attend(s, pend.pop(0)[0]))
                    for _ in ileave(gens, [1, 2] if len(gens) == 2 else None):
                        pass
                    if last and k < NTC:
                        qt_of.pop((s, k))
                while pend:
                    for _ in attend(s, pend.pop(0)[0]):
                        pass
            S.flush()

    def alloc_psum_pair(self, st):
        S = self.S
        items = []
        for i in range(2):
            t = self.palloc(st, 'apSY%d' % i, [128, 512], F32)
            items.append((t, S.res('apSY%d' % i)))
        return items

    def phase_lb(self):
        cfg, S = self.cfg, self.S
        dp = cfg.depth
        with ExitStack() as st:
            E = self.alloc(st, 'lbE', [128, dp, D], F32)
            Er = S.res('lbE')
            S.dma('sync', E[:].rearrange("p a b -> p (a b)"),
                  self.I['hgrn_lb_logits'].rearrange("a b -> (a b)").partition_broadcast(128), writes=[Er])
            S.act(E[:], E[:], AF.Exp, reads=[Er], writes=[Er])
            tot = self.alloc(st, 'lbtot', [128, D], F32)
            num = self.alloc(st, 'lbnum', [128, D], F32)
            lb = self.alloc(st, 'lblb', [128, 2, D], F32)
            tr_, nr_, lr_ = S.res('tot'), S.res('num'), S.res('lb')
            S.cp('vector', tot[:], E[:, 0, :], reads=[Er], writes=[tr_])
            for j in range(1, dp):
                S.tt('vector', tot[:], tot[:], E[:, j, :], ALU.add, reads=[Er, tr_], writes=[tr_])
            S.recip(tot[:], tot[:], reads=[tr_], writes=[tr_])
            for li in range(cfg.nh):
                layer = 2 * li + 1
                S.cp('vector', num[:], E[:, 1, :], reads=[Er], writes=[nr_])
                for j in range(2, layer + 1):
                    S.tt('vector', num[:], num[:], E[:, j, :], ALU.add, reads=[Er, nr_], writes=[nr_])
                S.tt('vector', lb[:, 0, :], num[:], tot[:], ALU.mult, reads=[nr_, tr_], writes=[lr_])
                S.ts('vector', lb[:, 1, :], lb[:, 0, :], -1.0, 1.0, ALU.mult, ALU.add, reads=[lr_], writes=[lr_])
                S.dma('sync', self.lbv[li:li + 1], lb[0:1, :, :], reads=[lr_], writes=[lr_])
            S.flush()

    def chain_gen(self, hg, bwd, QeT, QeTz, KinvT, Kdec, vb, Dd, rd, mask, maskr, Sst, Sr, SbA, SbAr, SbB, SbBr,
                  ATring, pA, pO, pD, need_o, evac):
        S = self.S
        hs = slice(hg * 4, (hg + 1) * 4)
        first, second = (1, 0) if bwd else (0, 1)
        S.cp('scalar', SbA[:, hs, :], Sst[:, hs, :], reads=[Sr], writes=[SbAr], accum=True)
        yield
        for idx, sub in enumerate((first, second)):
            rows = slice(sub * 64, (sub + 1) * 64)
            d, dr = pD.next()
            for h in range(4):
                hd = hg * 4 + h
                S.mm(d[:, h, :], Kdec[rows, hd * 128:(hd + 1) * 128], vb[rows, hd * 128:(hd + 1) * 128], True, True,
                     reads=rd, writes=[dr])
            S.tt('vector', Sst[:, hs, :], Sst[:, hs, :], Dd[:, hs, sub:sub + 1].to_broadcast([128, 4, 128]), ALU.mult,
                 reads=[Sr, SbAr, SbBr] + rd, writes=[Sr])
            S.tt('vector', Sst[:, hs, :], Sst[:, hs, :], d[:], ALU.add, reads=[Sr, dr], writes=[Sr])
            yield
            if idx == 0:
                S.cp('scalar', SbB[:, hs, :], Sst[:, hs, :], reads=[Sr], writes=[SbBr], accum=True)
                yield
        if need_o:
            a, ar = pA.next()
            for h in range(4):
                hd = hg * 4 + h
                S.mm(a[:, h, :], KinvT[:, hd, :], QeT[:, hd, :], True, True, reads=rd, writes=[ar])
            atm, atmr = ATring.next()
            S.tt('vector', atm[:], a[:], mask[:].unsqueeze(1).to_broadcast([128, 4, 128]), ALU.mult, reads=[ar, maskr],
                 writes=[atmr])
            yield
            o, orr = pO.next()
            S_lo, S_lor = (SbB, SbBr) if bwd else (SbA, SbAr)
            S_hi, S_hir = (SbA, SbAr) if bwd else (SbB, SbBr)
            for h in range(4):
                hd = hg * 4 + h
                S.mm(o[:, h, :], atm[:, h, :], vb[:, hd * 128:(hd + 1) * 128], True, False, reads=[atmr] + rd, writes=[orr])
                S.mm(o[0:64, h, :], QeT[:, hd, 0:64], S_lo[:, hd, :], False, False, reads=rd + [S_lor], writes=[orr])
                S.mm(o[:, h, :], QeTz[:, hd, :], S_hi[:, hd, :], False, True, reads=rd + [S_hir], writes=[orr])
            evac(hg, o, orr)
            yield

    def phase_hgrn(self, layer, j, last):
        self.phase_hf(layer, j, last)
        self.phase_hb(layer, j, last)

    def phase_hf(self, layer, j, last):
        cfg, S, nc = self.cfg, self.S, self.nc
        NTC, NT = cfg.NTC, cfg.NT
        with ExitStack() as st:
            self.common_consts(st)
            maskf, maskfr = self.load_const(st, 'm1f', BF16)
            m1 = [self.load_const(st, k, F32) for k in ('m1f', 'm1b')]
            m2 = [self.load_const(st, k, F32) for k in ('m2f', 'm2b')]
            c2 = [self.load_const(st, k, F32) for k in ('c2f', 'c2b')]
            win = self.alloc(st, 'hwin', [128, KC, 5120], BF16)
            winr = S.res('hwin')
            wsrc = self.I['hgrn_w_in'][j].rearrange("(kc p) n -> p kc n", p=128)
            for kc in range(KC):
                S.dma('gpsimd', win[:, kc, :], wsrc[:, kc, :], writes=[winr], accum=True)
            LBt = self.alloc(st, 'LBt', [128, 2, D], F32)
            LBr = S.res('LBt')
            S.dma('sync', LBt[:].rearrange("p a b -> p (a b)"),
                  self.lbv[j].rearrange("a b -> (a b)").partition_broadcast(128), writes=[LBr])
            modC = self.load_mod(st, layer, 2, 0, 'hmC', None, need='GS')
            modS = self.load_mod(st, layer, 0, 0, 'hmS', None, need='GS')
            Sst = self.alloc(st, 'Sst', [128, 8, 128], F32)
            Sr = S.res('Sst')
            SbA = self.alloc(st, 'SbA', [128, 8, 128], BF16)
            SbAr = S.res('SbA')
            SbB = self.alloc(st, 'SbB', [128, 8, 128], BF16)
            SbBr = S.res('SbB')
            qzring = self.ring(st, 'hqz', 2, [128, KC, 128], BF16)
            for qz_, qzr_ in qzring.items:
                S.memset('gpsimd', qz_[:], 0.0, writes=[qzr_])
            xring = self.ring(st, 'hx', 2, [128, D], F32)
            hbring = self.ring(st, 'hhb', 2, [128, D], BF16)
            string = self.ring(st, 'hst', 4, [128, 4], F32)
            hTring = self.ring(st, 'hhT', 2, [128, KC, 128], BF16)
            sqring = self.ring(st, 'hsq', 1, [128, D], F32)
            vbring = self.ring(st, 'hvb', 2, [128, D], BF16)
            sgring = self.ring(st, 'hsgt', 2, [128, D], BF16)
            fring = self.ring(st, 'hf', 4, [128, 512], F32)
            kkring = self.ring(st, 'hkk', 4, [128, 512], F32)
            exring = self.ring(st, 'hex', 4, [128, 512], F32)
            qering = [self.ring(st, 'hqe%d' % d, 1, [128, D], BF16) for d in range(2)]
            kiring = [self.ring(st, 'hki%d' % d, 1, [128, D], BF16) for d in range(2)]
            kdring = [self.ring(st, 'hkd%d' % d, 2 - d, [128, D], BF16) for d in range(2)]
            qeTring = [self.ring(st, 'hqeT%d' % d, 2 - d, [128, KC, 128], BF16) for d in range(2)]
            kiTring = [self.ring(st, 'hkiT%d' % d, 2 - d, [128, KC, 128], BF16) for d in range(2)]
            ddring = [self.ring(st, 'hdd%d' % d, 2, [128, 8, 2], F32) for d in range(2)]
            ATring = self.ring(st, 'hAT', 2, [128, 4, 128], BF16)
            ofring = self.ring(st, 'hof', 2, [128, 512], F32)
            ptr = self.ring(st, 'hptr', 1, [128, KC, 128], BF16, psum=True)
            pp = self.ring(st, 'hpp', 2, [128, 512], F32, psum=True)
            pE = self.ring(st, 'hpE', 2, [128, 512], F32, psum=True)
            pA = self.ring(st, 'hpA', 1, [128, 4, 128], F32, psum=True)
            pO = self.ring(st, 'hpO', 1, [128, 4, 128], F32, psum=True)
            pD = self.ring(st, 'hpD', 1, [128, 4, 128], F32, psum=True)

            def project(hT, hTr, col0):
                p, pr = pp.next()
                for kc in range(KC):
                    S.mm(p[:], hT[:, kc, :], win[:, kc, col0:col0 + 512], kc == 0, kc == KC - 1, reads=[hTr, winr], writes=[pr])
                return p, pr

            def sub_chain(d, half, hT, hTr, sq, sqr, qe, qer, ki, kir, kd, kdr, dd, ddr):
                hs = slice(half * 512, (half + 1) * 512)
                p, pr = project(hT, hTr, 1024 * (1 + d) + half * 512)
                f, fr = fring.next()
                S.act(f[:], p[:], AF.Sigmoid, reads=[pr], writes=[fr])
                yield
                fe = 'vector' if (d == 0 or half == 0) else 'gpsimd'
                S.tt(fe, f[:], f[:], LBt[:, 1, hs], ALU.mult, reads=[fr, LBr], writes=[fr])
                S.tt(fe, f[:], f[:], LBt[:, 0, hs], ALU.add, reads=[fr, LBr], writes=[fr])
                yield
                kk, kkr = kkring.next()
                S.ts('gpsimd', kk[:], f[:], -1.0, 1.0, ALU.mult, ALU.add, reads=[fr], writes=[kkr])
                S.act(f[:], f[:], AF.Ln, reads=[fr, kkr], writes=[fr])
                yield
                e1, e1r = pE.next()
                S.mm(e1[:], m1[d][0][:], f[:], True, True, reads=[m1[d][1], fr], writes=[e1r])
                ex, exr = exring.next()
                S.act(ex[:], e1[:], AF.Exp, reads=[e1r], writes=[exr])
                S.tt('vector', qe[:, hs], sq[:, hs], ex[:], ALU.mult, reads=[sqr, exr], writes=[qer], accum=True)
                ex, exr = exring.next()
                S.act(ex[:], e1[:], AF.Exp, reads=[e1r], writes=[exr], scale=-1.0)
                S.tt('gpsimd', ki[:, hs], kk[:], ex[:], ALU.mult, reads=[kkr, exr], writes=[kir], accum=True)
                yield
                e3, e3r = pE.next()
                S.mm(e3[:], m2[d][0][:], f[:], True, True, reads=[m2[d][1], fr], writes=[e3r])
                ex, exr = exring.next()
                S.act(ex[:], e3[:], AF.Exp, reads=[e3r], writes=[exr])
                S.tt('vector', kd[:, hs], kk[:], ex[:], ALU.mult, reads=[kkr, exr], writes=[kdr], accum=True)
                yield
                pd_, pdr = pE.next()
                for h in range(4):
                    S.mm(pd_[:, h * 2:h * 2 + 2], f[:, h * 128:(h + 1) * 128], c2[d][0][:], True, True,
                         reads=[fr, c2[d][1]], writes=[pdr])
                S.act(dd[:, half * 4:(half + 1) * 4, :], pd_[:, 0:8].rearrange("p (h c) -> p h c", c=2), AF.Exp,
                      reads=[pdr], writes=[ddr], accum=True)
                yield

            def load_x(s, t):
                xt, xr = xring.next()
                S.dma('sync', xt[:], self.x_src(s, t), writes=[xr])
                return xt, xr

            normed = {}

            def do_norm(s, t):
                mod = modC if t < NTC else modS
                if (s, t) not in pre_x:
                    pre_x[(s, t)] = load_x(s, t)
                xt, xr = pre_x.pop((s, t))
                hb, hbr = hbring.next()
                stt_, str_ = string.next()
                self.norm_tile(xt, xr, mod, stt_, str_, hb, hbr)
                normed[(s, t)] = (hb, hbr)
                i_ = tpos[(s, t)]
                if i_ + 1 < len(tiles) and tiles[i_ + 1][1] != 0:
                    pre_x[tiles[i_ + 1]] = load_x(*tiles[i_ + 1])

            def prep(s, t, nxt, res):
                if (s, t) not in normed:
                    do_norm(s, t)
                hb, hbr = normed.pop((s, t))
                yield
                pt, pr = ptr.next()
                for kc in range(KC):
                    S.tr(pt[:, kc, :], hb[:, kc * 128:(kc + 1) * 128], self.ident[:], reads=[hbr, self.identr], writes=[pr])
                hT, hTr = hTring.next()
                S.cp('scalar', hT[:], pt[:], reads=[pr], writes=[hTr])
                yield
                sq, sqr = sqring.next()
                vb, vbr = vbring.next()
                sg, sgr = sgring.next()
                bufs = []
                gens = []
                for d in range(2):
                    qe, qer = qering[d].next()
                    ki, kir = kiring[d].next()
                    kd, kdr = kdring[d].next()
                    dd, ddr = ddring[d].next()
                    bufs.append((qe, qer, ki, kir, kd, kdr, dd, ddr))
                    for half in range(2):
                        gens.append(sub_chain(d, half, hT, hTr, sq, sqr, qe, qer, ki, kir, kd, kdr, dd, ddr))
                for g_ in gens:
                    next(g_)
                    yield
                for half in range(2):
                    hs = slice(half * 512, (half + 1) * 512)
                    p, pr = project(hT, hTr, half * 512)
                    S.act(sq[:, hs], p[:], AF.Silu, reads=[pr], writes=[sqr], accum=True)
                    next(gens[2 * half])
                    yield
                    p, pr = project(hT, hTr, 4096 + half * 512)
                    S.act(sg[:, hs], p[:], AF.Silu, reads=[pr], writes=[sgr], accum=True)
                    next(gens[2 * half + 1])
                    yield
                    p, pr = project(hT, hTr, 3072 + half * 512)
                    S.cp('vector', vb[:, hs], p[:], reads=[pr], writes=[vbr], accum=True)
                    yield
                S.dma('sync', self.hv[s, t], vb[:], reads=[vbr])
                S.dma('sync', self.hsg[s, t], sg[:], reads=[sgr])
                yield from ileave(gens)
                if nxt is not None and nxt[1] != 0:
                    do_norm(*nxt)
                    yield
                ops = []
                for d in range(2):
                    qe, qer, ki, kir, kd, kdr, dd, ddr = bufs[d]
                    qeT, qeTr = qeTring[d].next()
                    kiT, kiTr = kiTring[d].next()
                    qz, qzr = (None, None)
                    for src, srcr, dst, dstr in ((qe, qer, qeT, qeTr), (ki, kir, kiT, kiTr)):
                        pt, pr = ptr.next()
                        for kc in range(KC):
                            S.tr(pt[:, kc, :], src[:, kc * 128:(kc + 1) * 128], self.ident[:], reads=[srcr, self.identr],
                                 writes=[pr])
                        S.cp('scalar', dst[:], pt[:], reads=[pr], writes=[dstr])
                        if d == 0 and src is qe:
                            qz, qzr = qzring.next()
                            S.cp('gpsimd', qz[:, :, 64:128], dst[:, :, 64:128], reads=[dstr], writes=[qzr])
                        yield
                    ops.append((qeT, qeTr, kiT, kiTr, kd, kdr, dd, ddr, qz, qzr))
                qeT, qeTr, kiT, kiTr, kd, kdr, dd, ddr, _, _ = ops[1]
                S.dma('sync', self.hqb[s, t], qeT[:], reads=[qeTr])
                S.dma('sync', self.hkb[s, t], kiT[:], reads=[kiTr])
                S.dma('sync', self.hkd[s, t], kd[:], reads=[kdr])
                S.dma('sync', self.hdb[s, t], dd[:].rearrange("p h c -> p (h c)"), reads=[ddr])
                res['ops'] = ops[0] + (vb, vbr)

            def chain(s, t, res):
                qeT, qeTr, kiT, kiTr, kd, kdr, dd, ddr, qz, qzr, vb, vbr = res['ops']
                rd = [qeTr, kiTr, kdr, ddr, vbr, qzr]
                if t == 0:
                    S.memset('vector', Sst[:], 0.0, writes=[Sr])
                need_o = not (last and t < NTC)

                def evac(hg, o, orr):
                    of, ofr = ofring.next()
                    S.cp('scalar', of[:], o[:].rearrange("p h v -> p (h v)"), reads=[orr], writes=[ofr])
                    S.dma('sync', self.hof[s, t, :, hg * 512:(hg + 1) * 512], of[:], reads=[ofr])

                gens = [self.chain_gen(hg, False, qeT, qz, kiT, kd, vb, dd, rd, maskf, maskfr, Sst, Sr, SbA, SbAr, SbB, SbBr,
                                       ATring, pA, pO, pD, need_o, evac) for hg in range(2)]
                yield from ileave(gens)

            tmpt, tmpr_ = xring.next()
            self.fill_mod(modC, layer, 2, 0, self.I['norm1_g'], tmpt, tmpr_)
            tiles = [(s, t) for s in range(cfg.NS) for t in range(NT)]
            prev = None
            tpos = {st_: i for i, st_ in enumerate(tiles)}
            pre_x = {}
            for i, (s, t) in enumerate(tiles):
                if t == 0:
                    tmpt, tmpr_ = xring.next()
                    self.fill_mod(modS, layer, s, 0, self.I['norm1_g'], tmpt, tmpr_)
                nxt = tiles[i + 1] if i + 1 < len(tiles) else None
                res = {}
                gens = [prep(s, t, nxt, res)]
                if prev is not None:
                    gens.append(chain(prev[0], prev[1], prev[2]))
                if os.environ.get('HF_SEQ', '0') == '1':
                    for g in reversed(gens):
                        for _ in g:
                            pass
                else:
                    for _ in ileave(gens, [3, 1]):
                        pass
                prev = (s, t, res)
            for _ in chain(prev[0], prev[1], prev[2]):
                pass
            S.flush()

    def phase_hb(self, layer, j, last):
        cfg, S, nc = self.cfg, self.S, self.nc
        NTC, NT = cfg.NTC, cfg.NT
        with ExitStack() as st:
            self.common_consts(st)
            maskb, maskbr = self.load_const(st, 'm1b', BF16)
            wout = self.alloc(st, 'hwout', [128, KC, D], BF16)
            woutr = S.res('hwout')
            S.dma('gpsimd', wout[:], self.I['hgrn_w_out'][j].rearrange("(kc p) n -> p kc n", p=128), writes=[woutr])
            OG = self.alloc(st, 'OG', [128, 128], F32)
            OGr = S.res('OG')
            S.dma('sync', OG[:, 0:1], self.I['hgrn_o_gain'][j].rearrange("(p o) -> p o", o=1), writes=[OGr])
            S.op('vector', lambda e: e.tensor_scalar(out=wout[:], in0=wout[:], scalar1=OG[:, 0:1], scalar2=None, op0=ALU.mult),
                 reads=[woutr, OGr], writes=[woutr])
            modC = self.load_mod(st, layer, 2, 0, 'bmC', None, need='A')
            modSs = [self.load_mod(st, layer, s, 0, 'bmS%d' % s, None, need='A') for s in range(cfg.NS)]
            states = []
            for s in range(cfg.NS):
                Sst = self.alloc(st, 'bSst%d' % s, [128, 8, 128], F32)
                SbA = self.alloc(st, 'bSbA%d' % s, [128, 8, 128], BF16)
                SbB = self.alloc(st, 'bSbB%d' % s, [128, 8, 128], BF16)
                states.append((Sst, S.res('Sst'), SbA, S.res('SbA'), SbB, S.res('SbB')))
            RS = []
            for s_ in range(cfg.NS):
                R = {}
                R['qz'] = self.ring(st, 'bqz%d' % s_, 2, [128, KC, 128], BF16)
                for qz_, qzr_ in R['qz'].items:
                    S.memset('gpsimd', qz_[:], 0.0, writes=[qzr_])
                R['x'] = self.ring(st, 'bx%d' % s_, 3, [128, D], F32)
                R['q'] = self.ring(st, 'bq%d' % s_, 2, [128, KC, 128], BF16)
                R['k'] = self.ring(st, 'bk%d' % s_, 2, [128, KC, 128], BF16)
                R['kd'] = self.ring(st, 'bkd%d' % s_, 2, [128, D], BF16)
                R['v'] = self.ring(st, 'bv%d' % s_, 2, [128, D], BF16)
                R['sg'] = self.ring(st, 'bsg%d' % s_, 3, [128, D], BF16)
                R['dd'] = self.ring(st, 'bdd%d' % s_, 2, [128, 8, 2], F32)
                R['of'] = self.ring(st, 'bof%d' % s_, 3, [128, D], F32)
                R['o'] = self.ring(st, 'bo%d' % s_, 2, [128, D], F32)
                R['sq'] = self.ring(st, 'bsq%d' % s_, 1, [128, D], F32)
                R['ss'] = self.ring(st, 'bss%d' % s_, 2, [128, 16], F32)
                R['ob'] = self.ring(st, 'bob%d' % s_, 2, [128, D], BF16)
                R['oT'] = self.ring(st, 'boT%d' % s_, 2, [128, KC, 128], BF16)
                R['AT'] = self.ring(st, 'bAT%d' % s_, 2, [128, 4, 128], BF16)
                R['tmp'] = self.ring(st, 'btmp%d' % s_, 2, [128, 512], F32)
                RS.append(R)
            ptr = self.ring(st, 'bptr', 1, [128, KC, 128], BF16, psum=True)
            pA = self.ring(st, 'bpA', 2, [128, 4, 128], F32, psum=True)
            pO = self.ring(st, 'bpO', 2, [128, 4, 128], F32, psum=True)
            pD = self.ring(st, 'bpD', 1, [128, 4, 128], F32, psum=True)
            py = self.ring(st, 'bpy', 1, [128, D], F32, psum=True)

            def loads(s, t):
                need_o = not (last and t < NTC)
                L = {}
                L['q'] = RS[s]['q'].next()
                L['k'] = RS[s]['k'].next()
                L['kd'] = RS[s]['kd'].next()
                L['v'] = RS[s]['v'].next()
                L['dd'] = RS[s]['dd'].next()
                S.dma('sync', L['q'][0][:], self.hqb[s, t], writes=[L['q'][1]])
                S.dma('sync', L['k'][0][:], self.hkb[s, t], writes=[L['k'][1]])
                S.dma('sync', L['kd'][0][:], self.hkd[s, t], writes=[L['kd'][1]])
                S.dma('sync', L['v'][0][:], self.hv[s, t], writes=[L['v'][1]])
                S.dma('sync', L['dd'][0][:].rearrange("p h c -> p (h c)"), self.hdb[s, t], writes=[L['dd'][1]])
                if need_o:
                    L['of'] = RS[s]['of'].next()
                    S.dma('sync', L['of'][0][:], self.hof[s, t], writes=[L['of'][1]])
                    L['sg'] = RS[s]['sg'].next()
                    S.dma('sync', L['sg'][0][:], self.hsg[s, t], writes=[L['sg'][1]])
                    L['x'] = RS[s]['x'].next()
                    S.dma('sync', L['x'][0][:], self.x_src(s, t), writes=[L['x'][1]])
                return L

            def chain_part(s, t, L):
                Sst, Sr, SbA, SbAr, SbB, SbBr = states[s]
                qT, qTr = L['q']
                kT, kTr = L['k']
                kd, kdr = L['kd']
                vb, vbr = L['v']
                dd, ddr = L['dd']
                qz, qzr = RS[s]['qz'].next()
                S.cp('gpsimd', qz[:, :, 64:128], qT[:, :, 64:128], reads=[qTr], writes=[qzr])
                yield
                rd = [qTr, kTr, kdr, vbr, ddr, qzr]
                need_o = not (last and t < NTC)
                o_sb, o_sbr = RS[s]['o'].next() if need_o else (None, None)
                of, ofr = L['of'] if need_o else (None, None)

                def evac(hg, o, orr):
                    hs = slice(hg * 512, (hg + 1) * 512)
                    S.tt('vector', o_sb[:, hs], o[:].rearrange("p h v -> p (h v)"), of[:, hs], ALU.add,
                         reads=[orr, ofr], writes=[o_sbr], accum=True)

                gens = [self.chain_gen(hg, True, qT, qz, kT, kd, vb, dd, rd, maskb, maskbr, Sst, Sr, SbA, SbAr, SbB, SbBr,
                                       RS[s]['AT'], pA, pO, pD, need_o, evac) for hg in range(2)]
                yield from ileave(gens)
                L['o_sb'] = (o_sb, o_sbr)

            def readout_part(s, t, L):
                mod = modC if t < NTC else modSs[s]
                need_o = not (last and t < NTC)
                if not need_o:
                    return
                o_sb, o_sbr = L['o_sb']
                sg, sgr = L['sg']
                sq, sqr = RS[s]['sq'].next()
                S.act(sq[:], o_sb[:], AF.Square, reads=[o_sbr], writes=[sqr])
                yield
                ss, ssr = RS[s]['ss'].next()
                S.red(ss[:, 0:8], sq[:].rearrange("p (h v) -> p h v", v=128), ALU.add, reads=[sqr], writes=[ssr])
                S.act(ss[:, 8:16], ss[:, 0:8], AF.Ln, reads=[ssr, self.epsr], writes=[ssr], scale=1.0 / 128, bias=self.epsb[:, 0:1])
                S.act(ss[:, 8:16], ss[:, 8:16], AF.Exp, reads=[ssr], writes=[ssr], scale=-0.5)
                yield
                ob, obr = RS[s]['ob'].next()
                for h in range(8):
                    hs_ = slice(h * 128, (h + 1) * 128)
                    S.stt('vector', ob[:, hs_], o_sb[:, hs_], ss[:, 8 + h:9 + h], sg[:, hs_], ALU.mult, ALU.mult,
                          reads=[o_sbr, ssr, sgr], writes=[obr], accum=True)
                yield
                pt, pr = ptr.next()
                for kc in range(KC):
                    S.tr(pt[:, kc, :], ob[:, kc * 128:(kc + 1) * 128], self.ident[:], reads=[obr, self.identr], writes=[pr])
                oT, oTr = RS[s]['oT'].next()
                S.cp('scalar', oT[:], pt[:], reads=[pr], writes=[oTr])
                yield
                y, yr = py.next()
                xt, xr = L['x']
                G, SH, GA, mr = mod
                for half in range(2):
                    for kc in range(KC):
                        S.mm(y[:, half * 512:(half + 1) * 512], oT[:, kc, :], wout[:, kc, half * 512:(half + 1) * 512],
                             kc == 0, kc == KC - 1, reads=[oTr, woutr], writes=[yr])
                for half in range(2):
                    tmp, tmpr = RS[s]['tmp'].next()
                    S.tt('vector', tmp[:], y[:, half * 512:(half + 1) * 512], GA[:, half * 512:(half + 1) * 512], ALU.mult,
                         reads=[yr, mr], writes=[tmpr])
                    S.tt('gpsimd', xt[:, half * 512:(half + 1) * 512], tmp[:], xt[:, half * 512:(half + 1) * 512],
                         ALU.add, reads=[tmpr, xr], writes=[xr])
                yield
                S.dma('sync', self.x_dst(s, t), xt[:], reads=[xr])
                self.written.add((s, t))

            def seq(s):
                Sst, Sr = states[s][0], states[s][1]
                S.memset('vector', Sst[:], 0.0, writes=[Sr])
                order = list(range(NTC - 1, -1, -1)) + list(range(NT - 1, NTC - 1, -1))
                Ls = {0: loads(s, order[0])}
                if len(order) > 1:
                    Ls[1] = loads(s, order[1])
                yield from chain_part(s, order[0], Ls[0])
                for i, t in enumerate(order):
                    if i + 2 < len(order):
                        Ls[i + 2] = loads(s, order[i + 2])
                    gens = [readout_part(s, t, Ls[i])]
                    if i + 1 < len(order):
                        gens.append(chain_part(s, order[i + 1], Ls[i + 1]))
                    yield from ileave(gens)
                    del Ls[i]

            self.fill_mod(modC, layer, 2, 0, self.I['norm1_g'], None, None)
            for s in range(cfg.NS):
                self.fill_mod(modSs[s], layer, s, 0, self.I['norm1_g'], None, None)
            for _ in ileave([seq(s) for s in range(cfg.NS)]):
                pass
            S.flush()


def ileave(gens, weights=None):
    items = [[g, (weights[i] if weights else 1)] for i, g in enumerate(gens)]
    while items:
        for it in list(items):
            for _ in range(it[1]):
                try:
                    next(it[0])
                except StopIteration:
                    items.remove(it)
                    break
                yield


_CACHE = {}


def run(inputs, cfg, ncores):
    key = (cfg.L, cfg.CTX, cfg.depth, cfg.NS, cfg.do_mixer, cfg.do_ffn)
    if key not in _CACHE:
        _CACHE[key] = Builder(cfg).build()
    nc = _CACHE[key]
    consts = host_consts(cfg)
    in_maps = []
    for i in range(ncores):
        m = {}
        sl = slice(i * cfg.NS, (i + 1) * cfg.NS)
        for k in input_shapes(cfg):
            a = np.asarray(inputs[k], dtype=np.float32)
            if k in ('x', 'c', 'ctx'):
                a = a[sl]
            m[k] = np.ascontiguousarray(a)
        for k, v in consts.items():
            m['k_' + k] = v
        in_maps.append(m)
    res = run_bass_kernel_spmd(nc, in_maps, core_ids=list(range(ncores)))
    if getattr(cfg, 'debug', False):
        global DEBUG_OUT
        DEBUG_OUT = res.results
    return np.concatenate([np.asarray(r["out"]) for r in res.results], axis=0).astype(np.float32)


def kernel(**inputs):
    cfg = Cfg()
    return run(inputs, cfg, 8)
```

```python
import math
from contextlib import ExitStack
import numpy as np
import concourse.bass as bass
import concourse.mybir as mybir
from concourse.bass_utils import run_bass_kernel_spmd

F32 = mybir.dt.float32
BF16 = mybir.dt.bfloat16
AF = mybir.ActivationFunctionType
ALU = mybir.AluOpType
AX = mybir.AxisListType

ENGS = ['sync', 'scalar', 'gpsimd', 'vector', 'tensor']
NDS = 8
D = 1024
KC = 8
DFF = 2816
NFC = 22
EPS = 1e-6
import os
CH_ENG = os.environ.get('CH_ENG', 'gpsimd')


class Res:
    __slots__ = ('name', 'w', 'r', 'hr')

    def __init__(self, name=''):
        self.name = name
        self.w = []
        self.r = {}
        self.hr = False


class Op:
    __slots__ = ('eng', 'fn', 'deps', 'need_inc', 'is_dma', 'sem', 'val')


class Sched:
    def __init__(self, nc, stack):
        self.nc = nc
        self.ops = {e: [] for e in ENGS}
        self.esem = {e: stack.enter_context(nc.semaphore('es_' + e)) for e in ENGS}
        self.dsem = {e: [stack.enter_context(nc.semaphore('ds_%s%d' % (e, i))) for i in range(NDS)]
                     for e in ENGS}
        self.dcnt = {e: 0 for e in ENGS}
        self.dlast = {e: {} for e in ENGS}
        self.ecnt = {e: 0 for e in ENGS}
        self.waited = {e: {} for e in ENGS}
        self.allres = []
        self.ninst = 0

    def res(self, name=''):
        r = Res(name)
        self.allres.append(r)
        return r

    def op(self, eng, fn, reads=(), writes=(), is_dma=False, accum=False):
        o = Op()
        o.eng = eng
        o.fn = fn
        o.need_inc = False
        o.is_dma = is_dma
        o.sem = None
        o.val = 0
        deps = {}
        for r in reads:
            for d in r.w:
                deps[id(d)] = d
        same_gen = {}
        for w in writes:
            sg = accum and (not w.hr) and len(w.w) > 0
            same_gen[id(w)] = sg
            if not sg:
                for d in w.w:
                    deps[id(d)] = d
            for d in w.r.values():
                if isinstance(d, list):
                    for dd in d:
                        deps[id(dd)] = dd
                else:
                    deps[id(d)] = d
        for r in reads:
            r.hr = True
            if is_dma:
                r.r.setdefault('dma', []).append(o)
            else:
                r.r[eng] = o
        for w in writes:
            if same_gen[id(w)]:
                w.w.append(o)
            else:
                w.w = [o]
                w.r = {}
                w.hr = False
        dl = []
        for d in deps.values():
            if d is o:
                continue
            if d.eng == eng and eng == 'tensor' and not d.is_dma and not is_dma:
                continue
            dl.append(d)
        o.deps = dl
        if is_dma:
            i = self.dcnt[eng]
            self.dcnt[eng] += 1
            o.sem = self.dsem[eng][i % NDS]
            o.val = 16 * (i // NDS + 1)
            prev = self.dlast[eng].get(i % NDS)
            if prev is not None and all(d is not prev for d in dl):
                dl.append(prev)
            self.dlast[eng][i % NDS] = o
        self.ops[eng].append(o)
        return o

    def dma(self, q, out, in_, reads=(), writes=(), accum=False, **kw):
        return self.op(q, lambda e: e.dma_start(out=out, in_=in_, **kw), reads, writes, is_dma=True, accum=accum)

    def mm(self, out, lhsT, rhs, start, stop, reads=(), writes=(), accum=False):
        return self.op('tensor', lambda e: e.matmul(out, lhsT=lhsT, rhs=rhs, start=start, stop=stop),
                       reads, writes, accum=accum)

    def tr(self, out, in_, ident, reads=(), writes=(), accum=False):
        return self.op('tensor', lambda e: e.transpose(out=out, in_=in_, identity=ident), reads, writes, accum=accum)

    def act(self, out, in_, func, reads=(), writes=(), accum=False, **kw):
        return self.op('scalar', lambda e: e.activation(out=out, in_=in_, func=func, **kw), reads, writes, accum=accum)

    def tt(self, eng, out, in0, in1, op, reads=(), writes=(), accum=False):
        return self.op(eng, lambda e: e.tensor_tensor(out=out, in0=in0, in1=in1, op=op), reads, writes, accum=accum)

    def ts(self, eng, out, in0, s1, s2, op0, op1=None, reads=(), writes=(), accum=False):
        if op1 is None:
            return self.op(eng, lambda e: e.tensor_scalar(out=out, in0=in0, scalar1=s1, scalar2=None, op0=op0),
                           reads, writes, accum=accum)
        return self.op(eng, lambda e: e.tensor_scalar(out=out, in0=in0, scalar1=s1, scalar2=s2, op0=op0, op1=op1),
                       reads, writes, accum=accum)

    def stt(self, eng, out, in0, scalar, in1, op0, op1, reads=(), writes=(), accum=False):
        return self.op(eng, lambda e: e.scalar_tensor_tensor(out=out, in0=in0, scalar=scalar, in1=in1,
                                                             op0=op0, op1=op1), reads, writes, accum=accum)

    def cp(self, eng, out, in_, reads=(), writes=(), accum=False):
        if eng == 'scalar':
            return self.op(eng, lambda e: e.copy(out=out, in_=in_), reads, writes, accum=accum)
        return self.op(eng, lambda e: e.tensor_copy(out=out, in_=in_), reads, writes, accum=accum)

    def memset(self, eng, ap, val, writes=(), accum=False):
        return self.op(eng, lambda e: e.memset(ap, val), (), writes, accum=accum)

    def red(self, out, in_, op, reads=(), writes=()):
        return self.op('vector', lambda e: e.tensor_reduce(out=out, in_=in_, axis=AX.X, op=op), reads, writes)

    def recip(self, out, in_, reads=(), writes=()):
        return self.op('vector', lambda e: e.reciprocal(out=out, in_=in_), reads, writes)

    def flush(self):
        for e in ENGS:
            for o in self.ops[e]:
                for d in o.deps:
                    if not d.is_dma:
                        d.need_inc = True
        for e in ENGS:
            lst = [o for o in self.ops[e] if not o.is_dma]
            if lst:
                lst[-1].need_inc = True
            c = self.ecnt[e]
            for o in self.ops[e]:
                if o.need_inc and not o.is_dma:
                    c += 1
                    o.sem = self.esem[e]
                    o.val = c
            self.ecnt[e] = c
        finals = []
        for e in ENGS:
            if self.ecnt[e] > 0:
                finals.append((self.esem[e], self.ecnt[e]))
            n = self.dcnt[e]
            for i in range(min(n, NDS)):
                cnt = (n - i + NDS - 1) // NDS
                finals.append((self.dsem[e][i], 16 * cnt))
        with self.nc.Block() as block:
            for e in ENGS:
                def body(engine, e=e):
                    waited = self.waited[e]
                    for o in self.ops[e]:
                        for d in o.deps:
                            if waited.get(d.sem.num, 0) >= d.val:
                                continue
                            engine.wait_ge(d.sem, d.val)
                            waited[d.sem.num] = d.val
                            self.ninst += 1
                        ins = o.fn(engine)
                        self.ninst += 1
                        if o.is_dma:
                            ins.then_inc(o.sem, 16)
                        elif o.need_inc:
                            ins.then_inc(o.sem, 1)
                    for sem, val in finals:
                        if waited.get(sem.num, 0) >= val:
                            continue
                        engine.wait_ge(sem, val)
                        waited[sem.num] = val
                getattr(block, e)(body)
        self.ops = {e: [] for e in ENGS}
        self.dlast = {e: {} for e in ENGS}
        for r in self.allres:
            r.w = []
            r.r = {}
            r.hr = False
        self.allres = []


class Ring:
    def __init__(self, items):
        self.items = items
        self.i = 0

    def next(self):
        it = self.items[self.i % len(self.items)]
        self.i += 1
        return it


class Cfg:
    def __init__(self, L=4096, CTX=256, depth=4, NS=2, do_mixer=True, do_ffn=True):
        self.L = L
        self.CTX = CTX
        self.depth = depth
        self.NS = NS
        self.NTC = CTX // 128
        self.NTX = L // 128
        self.NT = self.NTC + self.NTX
        self.TOK = CTX + L
        self.na = (depth + 1) // 2
        self.nh = depth // 2
        self.do_mixer = do_mixer
        self.do_ffn = do_ffn


def host_consts(cfg):
    c = {}
    i = np.arange(128)
    c['ident'] = np.eye(128, dtype=np.float32)
    s = i[:, None]
    t = i[None, :]
    c['mprev'] = (s >= t).astype(np.float32)
    c['mnext'] = (s <= t).astype(np.float32)
    same = ((s // 64) == (t // 64)).astype(np.float32)
    c['m1f'] = (s <= t).astype(np.float32) * same
    c['m2f'] = (s > t).astype(np.float32) * same
    c['m1b'] = (s >= t).astype(np.float32) * same
    c['m2b'] = (s < t).astype(np.float32) * same
    c['c2f'] = np.stack([(i < 64), (i >= 64)], 1).astype(np.float32)
    c['c2b'] = c['c2f'].copy()
    L = cfg.L
    pos = np.arange(L)
    row = (pos // 64).astype(np.float32)
    col = (pos % 64).astype(np.float32)
    inv = (10000.0 ** (-np.arange(16, dtype=np.float32) / 16)).astype(np.float32)
    ang = np.concatenate([row[:, None] * inv, col[:, None] * inv], 1).astype(np.float32)
    c['rcos'] = np.cos(ang).astype(np.float32)
    c['rsin'] = np.sin(ang).astype(np.float32)
    return c


CONST_SHAPES = lambda cfg: {
    'ident': [128, 128], 'mprev': [128, 128], 'mnext': [128, 128], 'm1f': [128, 128], 'm2f': [128, 128],
    'm1b': [128, 128], 'm2b': [128, 128], 'c2f': [128, 2], 'c2b': [128, 2],
    'rcos': [cfg.L, 32], 'rsin': [cfg.L, 32]}


def input_shapes(cfg):
    dp, na, nh = cfg.depth, cfg.na, cfg.nh
    return {
        'x': [cfg.NS, cfg.L, D], 'c': [cfg.NS, D], 'ctx': [cfg.NS, cfg.CTX, D], 'c_ctx': [D],
        'ada_w': [dp, D, 6 * D], 'ada_b': [dp, 6 * D], 'norm1_g': [dp, D], 'norm2_g': [dp, D],
        'attn_w_in': [na, D, 1536], 'attn_w_out': [na, D, D], 'attn_q_gain': [na, 64],
        'attn_k_gain': [na, 64], 'attn_sink': [na, 16],
        'hgrn_w_in': [max(nh, 1), D, 5120], 'hgrn_w_out': [max(nh, 1), D, D], 'hgrn_o_gain': [max(nh, 1), 128],
        'hgrn_lb_logits': [dp, 1024],
        'ffn_w_up': [dp, D, 2 * DFF], 'ffn_conv_w': [dp, 3, DFF], 'ffn_conv_b': [dp, DFF],
        'ffn_w_down': [dp, DFF, D]}


class Builder:
    def __init__(self, cfg):
        self.cfg = cfg
        nc = bass.Bass("TRN2", target_bir_lowering=False)
        self.nc = nc
        self.I = {k: nc.dram_tensor(k, shp, F32, kind="ExternalInput").ap() for k, shp in input_shapes(cfg).items()}
        self.C = {k: nc.dram_tensor('k_' + k, shp, F32, kind="ExternalInput").ap()
                  for k, shp in CONST_SHAPES(cfg).items()}
        self.out = nc.dram_tensor("out", [cfg.NS, cfg.L, D], F32, kind="ExternalOutput").ap()
        self.xs = nc.dram_tensor("xs", [cfg.NS, cfg.TOK, D], F32, kind="Internal").ap()
        self.modv = nc.dram_tensor("modv", [cfg.depth, 3, 6 * D], F32, kind="Internal").ap()
        self.written = set()
        if cfg.nh > 0:
            NS, NT = cfg.NS, cfg.NT
            dkind = "ExternalOutput" if getattr(cfg, 'debug', False) else "Internal"
            dt = lambda n, shp, d: nc.dram_tensor(n, shp, d, kind=dkind).ap()
            self.lbv = dt("lbv", [cfg.nh, 2, D], F32)
            self.hqb = dt("hqb", [NS, NT, 128, KC, 128], BF16)
            self.hkb = dt("hkb", [NS, NT, 128, KC, 128], BF16)
            self.hkd = dt("hkd", [NS, NT, 128, D], BF16)
            self.hv = dt("hv", [NS, NT, 128, D], BF16)
            self.hsg = dt("hsg", [NS, NT, 128, D], BF16)
            self.hdb = dt("hdb", [NS, NT, 128, 16], F32)
            self.hof = dt("hof", [NS, NT, 128, D], F32)

    def x_src(self, s, t):
        cfg = self.cfg
        if (s, t) in self.written:
            return self.xs[s, t * 128:(t + 1) * 128, :]
        if t < cfg.NTC:
            return self.I['ctx'][s, t * 128:(t + 1) * 128, :]
        tt = t - cfg.NTC
        return self.I['x'][s, tt * 128:(tt + 1) * 128, :]

    def x_dst(self, s, t, final=False):
        cfg = self.cfg
        if final:
            tt = t - cfg.NTC
            return self.out[s, tt * 128:(tt + 1) * 128, :]
        return self.xs[s, t * 128:(t + 1) * 128, :]

    def build(self):
        cfg = self.cfg
        nc = self.nc
        with ExitStack() as top:
            self.S = Sched(nc, top)
            self.phase_mod()
            if cfg.nh > 0 and cfg.do_mixer:
                self.phase_lb()
            for layer in range(cfg.depth):
                last = layer == cfg.depth - 1
                if cfg.do_mixer:
                    if layer % 2 == 0:
                        self.phase_attn(layer, layer // 2, last)
                    else:
                        self.phase_hgrn(layer, layer // 2, last)
                if cfg.do_ffn:
                    self.phase_ffn(layer, last)
        return nc

    def alloc(self, st, name, shape, dt):
        self.uid = getattr(self, 'uid', 0) + 1
        return st.enter_context(self.nc.sbuf_tensor('%s_%d' % (name, self.uid), shape, dt))

    def palloc(self, st, name, shape, dt):
        self.uid = getattr(self, 'uid', 0) + 1
        return st.enter_context(self.nc.psum_tensor('%s_%d' % (name, self.uid), shape, dt))

    def ring(self, st, name, n, shape, dt, psum=False):
        S = self.S
        items = []
        for i in range(n):
            t = (self.palloc if psum else self.alloc)(st, '%s%d' % (name, i), shape, dt)
            items.append((t, S.res('%s%d' % (name, i))))
        return Ring(items)

    def load_const(self, st, key, dt, q='sync'):
        S = self.S
        shp = list(self.C[key].shape)
        t = self.alloc(st, 'c_' + key, shp, dt)
        r = S.res(key)
        S.dma('gpsimd' if dt != F32 else q, t[:], self.C[key], writes=[r])
        return t, r

    def load_mod(self, st, layer, rset, sub, name, norm_g, need='GSA'):
        S = self.S
        G = self.alloc(st, name + 'G', [128, D], F32) if 'G' in need else None
        SH = self.alloc(st, name + 'SH', [128, D], F32) if 'S' in need else None
        GA = self.alloc(st, name + 'GA', [128, D], F32) if 'A' in need else None
        r = S.res(name)
        return (G, SH, GA, r)

    def fill_mod(self, mod, layer, rset, sub, norm_g, tmp, tmpr):
        S = self.S
        G, SH, GA, r = mod
        base = 3 * sub * D
        mv = self.modv
        if SH is not None:
            S.dma('sync', SH[:], mv[layer, rset, base:base + D].partition_broadcast(128), writes=[r], accum=True)
        if GA is not None:
            S.dma('sync', GA[:], mv[layer, rset, base + 2 * D:base + 3 * D].partition_broadcast(128), writes=[r], accum=True)
        if G is not None:
            S.dma('sync', G[:], mv[layer, rset, base + D:base + 2 * D].partition_broadcast(128), writes=[r], accum=True)
            S.dma('sync', tmp[:], norm_g[layer].partition_broadcast(128), writes=[tmpr])
            S.stt('vector', G[:], G[:], 1.0, tmp[:], ALU.add, ALU.mult, reads=[r, tmpr], writes=[r])

    def norm_tile(self, xt, xr, mod, stat, statr, hb, hbr):
        S = self.S
        G, SH, GA, mr = mod
        S.act(hb[:], xt[:], AF.Square, reads=[xr], writes=[hbr, statr], accum_out=stat[:, 0:1])
        S.act(stat[:, 1:2], stat[:, 0:1], AF.Ln, reads=[statr, self.epsr], writes=[statr], scale=1.0 / D, bias=self.epsb[:, 0:1])
        S.act(stat[:, 2:3], stat[:, 1:2], AF.Exp, reads=[statr], writes=[statr], scale=-0.5)
        S.stt('vector', xt[:], xt[:], stat[:, 2:3], G[:], ALU.mult, ALU.mult, reads=[xr, statr, mr], writes=[xr])
        S.tt('vector', hb[:], xt[:], SH[:], ALU.add, reads=[xr, mr], writes=[hbr])

    def common_consts(self, st):
        S = self.S
        self.ident, self.identr = self.load_const(st, 'ident', BF16)
        self.epsb = self.alloc(st, 'epsb', [128, 1], F32)
        self.epsr = S.res('epsb')
        S.memset('vector', self.epsb[:], EPS, writes=[self.epsr])

    def phase_mod(self):
        cfg, S, nc = self.cfg, self.S, self.nc
        with ExitStack() as st:
            cT = self.alloc(st, 'cT', [128, 3, 8], F32)
            cTr = S.res('cT')
            scb = self.alloc(st, 'scb', [128, 3, 8], F32)
            scbr = S.res('scb')
            for r in range(cfg.NS):
                S.dma('sync', cT[:, r, :], self.I['c'][r].rearrange("(p kc) -> p kc", kc=8), writes=[cTr], accum=True)
            if cfg.NS < 2:
                S.dma('sync', cT[:, 1, :], self.I['c'][0].rearrange("(p kc) -> p kc", kc=8), writes=[cTr], accum=True)
            S.dma('sync', cT[:, 2, :], self.I['c_ctx'].rearrange("(p kc) -> p kc", kc=8), writes=[cTr], accum=True)
            S.act(scb[:], cT[:], AF.Silu, reads=[cTr], writes=[scbr])
            wring = self.ring(st, 'adaw', 4, [128, 8, 512], F32)
            bring = self.ring(st, 'adab', 2, [3, 6 * D], F32)
            pring = self.ring(st, 'pmod', 2, [3, 512], F32, psum=True)
            mring = self.ring(st, 'mrow', 2, [3, 512], F32)
            qi = 0
            for layer in range(cfg.depth):
                bt, br = bring.next()
                S.dma('sync', bt[:], self.I['ada_b'][layer].partition_broadcast(3), writes=[br])
                wv = self.I['ada_w'][layer].rearrange("(p kc) n -> p kc n", kc=8)
                for nb in range(12):
                    wt, wr = wring.next()
                    S.dma(('sync', 'scalar')[qi % 2], wt[:], wv[:, :, nb * 512:(nb + 1) * 512], writes=[wr])
                    qi += 1
                    pt, pr = pring.next()
                    for kc in range(8):
                        S.mm(pt[:], scb[:, :, kc], wt[:, kc, :], kc == 0, kc == 7, reads=[scbr, wr], writes=[pr])
                    mt, mr = mring.next()
                    S.tt('vector', mt[:], pt[:], bt[:, nb * 512:(nb + 1) * 512], ALU.add, reads=[pr, br], writes=[mr])
                    S.dma('sync', self.modv[layer, :, nb * 512:(nb + 1) * 512], mt[:], reads=[mr])
            S.flush()

    def phase_ffn(self, layer, last):
        cfg, S, nc = self.cfg, self.S, self.nc
        with ExitStack() as st:
            self.common_consts(st)
            wup = self.alloc(st, 'wup', [128, KC, 2 * DFF], BF16)
            wupr = S.res('wup')
            wdn = self.alloc(st, 'wdn', [128, NFC, D], BF16)
            wdnr = S.res('wdn')
            wu = self.I['ffn_w_up'][layer].rearrange("(kc p) n -> p kc n", p=128)
            for kc in range(KC):
                S.dma('gpsimd', wup[:, kc, :], wu[:, kc, :], writes=[wupr], accum=True)
            wd = self.I['ffn_w_down'][layer].rearrange("(c p) n -> p c n", p=128)
            for c0 in range(0, NFC, 6):
                c1 = min(NFC, c0 + 6)
                S.dma('gpsimd', wdn[:, c0:c1, :], wd[:, c0:c1, :], writes=[wdnr], accum=True)
            cw = self.alloc(st, 'cw', [128, 3, NFC], F32)
            cb = self.alloc(st, 'cb', [128, NFC], F32)
            cwr = S.res('cw')
            for j in range(3):
                S.dma('sync', cw[:, j, :], self.I['ffn_conv_w'][layer, j].rearrange("(c p) -> p c", p=128),
                      writes=[cwr], accum=True, allow_slow_non_contiguous=True)
            S.dma('sync', cb[:], self.I['ffn_conv_b'][layer].rearrange("(c p) -> p c", p=128), writes=[cwr], accum=True,
                  allow_slow_non_contiguous=True)
            mod = self.load_mod(st, layer, 0, 1, 'fm', None)
            xring = self.ring(st, 'fx', 2, [128, D], F32)
            xdring = self.ring(st, 'fxd', 1, [128, D], F32)
            pre_x = {}
            hbring = self.ring(st, 'fhb', 2, [128, D], BF16)
            string = self.ring(st, 'fst', 4, [128, 4], F32)
            hT = [self.alloc(st, 'fhT%d' % i, [128, KC, 514], BF16) for i in range(2)]
            hTr = [[S.res('hT%d_%d' % (i, j)) for j in range(4)] for i in range(2)]
            hTL = [S.res('hTL%d' % i) for i in range(2)]
            hTR = [S.res('hTR%d' % i) for i in range(2)]
            actT = self.alloc(st, 'actT', [128, NFC, 512], BF16)
            actr = [S.res('act%d' % c) for c in range(NFC)]
            gbring = self.ring(st, 'gb', 2, [128, 514], F32)
            t1ring = self.ring(st, 't1', 2, [128, 512], F32)
            ptr = self.ring(st, 'ptr', 1, [128, KC, 128], BF16, psum=True)
            pg = self.ring(st, 'pg', 2, [128, 512], F32, psum=True)
            pv = self.ring(st, 'pv', 2, [128, 512], F32, psum=True)
            ph = self.ring(st, 'ph', 1, [128, 512], F32, psum=True)
            py = self.ring(st, 'py', 2, [128, 512], F32, psum=True)
            phi = [0]

            def prefetch(s, t):
                xt, xr = xring.next()
                S.dma('sync', xt[:], self.x_src(s, t), writes=[xr])
                pre_x[(s, t)] = (xt, xr)

            def prepA(s, t):
                if (s, t) not in pre_x:
                    prefetch(s, t)
                xt, xr = pre_x.pop((s, t))
                hb, hbr = hbring.next()
                stt_, str_ = string.next()
                self.norm_tile(xt, xr, mod, stt_, str_, hb, hbr)
                return hb, hbr

            def prepB(hb, hbr, buf, slot, left_to, right_to):
                pt, pr = ptr.next()
                for kc in range(KC):
                    S.tr(pt[:, kc, :], hb[:, kc * 128:(kc + 1) * 128], self.ident[:], reads=[hbr, self.identr],
                         writes=[pr])
                S.cp('scalar', hT[buf][:, :, 1 + slot * 128:1 + (slot + 1) * 128], pt[:], reads=[pr],
                     writes=[hTr[buf][slot]])
                if left_to is not None:
                    S.cp('scalar', hT[left_to][:, :, 0:1], pt[:, :, 127:128], reads=[pr], writes=[hTL[left_to]])
                if right_to is not None:
                    b, col = right_to
                    S.cp('scalar', hT[b][:, :, col:col + 1], pt[:, :, 0:1], reads=[pr], writes=[hTR[b]])

            def up(buf, nt):
                n = nt * 128
                for c in range(NFC):
                    g, gr = pg.next()
                    v, vr = pv.next()
                    h, hr = ph.next()
                    hc = (phi[0] % 16) * 2
                    phi[0] += 1
                    rd = [wupr] + hTr[buf][:nt]
                    for kc in range(KC):
                        S.mm(g[:, 0:n], wup[:, kc, c * 128:(c + 1) * 128], hT[buf][:, kc, 1:1 + n], kc == 0, kc == KC - 1,
                             reads=rd, writes=[gr])
                    for kc in range(KC):
                        S.mm(h[:, hc:hc + 2], wup[:, kc, c * 128:(c + 1) * 128], hT[buf][:, kc, 0:n + 2:n + 1],
                             kc == 0, kc == KC - 1, reads=[wupr, hTL[buf], hTR[buf]], writes=[hr])
                    for kc in range(KC):
                        S.mm(v[:, 0:n], wup[:, kc, DFF + c * 128:DFF + (c + 1) * 128], hT[buf][:, kc, 1:1 + n],
                             kc == 0, kc == KC - 1, reads=rd, writes=[vr])
                    gb, gbr = gbring.next()
                    S.cp('scalar', gb[:, 1:1 + n], g[:, 0:n], reads=[gr], writes=[gbr])
                    S.cp('scalar', gb[:, 0:n + 2:n + 1], h[:, hc:hc + 2], reads=[hr], writes=[gbr])
                    t1, t1r = t1ring.next()
                    S.ts('vector', t1[:, 0:n], gb[:, 1:1 + n], cw[:, 1, c:c + 1], cb[:, c:c + 1], ALU.mult, ALU.add,
                         reads=[gbr, cwr], writes=[t1r])
                    S.stt('vector', t1[:, 0:n], gb[:, 0:n], cw[:, 0, c:c + 1], t1[:, 0:n], ALU.mult, ALU.add,
                          reads=[gbr, cwr, t1r], writes=[t1r])
                    S.stt('vector', t1[:, 0:n], gb[:, 2:2 + n], cw[:, 2, c:c + 1], t1[:, 0:n], ALU.mult, ALU.add,
                          reads=[gbr, cwr, t1r], writes=[t1r])
                    S.act(t1[:, 0:n], t1[:, 0:n], AF.Silu, reads=[t1r], writes=[t1r])
                    S.tt('vector', actT[:, c, 0:n], t1[:, 0:n], v[:, 0:n], ALU.mult, reads=[t1r, vr], writes=[actr[c]])
                    yield

            def down(s, tiles, final):
                G, SH, GA, mr = mod
                for j, t in enumerate(tiles):
                    xt, xr = xdring.next()
                    S.dma('sync', xt[:], self.x_src(s, t), writes=[xr])
                    for half in range(2):
                        y, yr = py.next()
                        for c in range(NFC):
                            S.mm(y[:], actT[:, c, j * 128:(j + 1) * 128], wdn[:, c, half * 512:(half + 1) * 512],
                                 c == 0, c == NFC - 1, reads=[actr[c], wdnr], writes=[yr])
                        tmp, tmpr = t1ring.next()
                        S.tt('vector', tmp[:], y[:], GA[:, half * 512:(half + 1) * 512], ALU.mult, reads=[yr, mr],
                             writes=[tmpr])
                        S.tt('vector', xt[:, half * 512:(half + 1) * 512], tmp[:], xt[:, half * 512:(half + 1) * 512],
                             ALU.add, reads=[tmpr, xr], writes=[xr])
                    S.dma('sync', self.x_dst(s, t, final), xt[:], reads=[xr])
                    self.written.add((s, t))
                    yield

            bufi = [0]
            for s in range(cfg.NS):
                segs = []
                if not last:
                    segs.append((2, list(range(cfg.NTC))))
                segs.append((s, list(range(cfg.NTC, cfg.NT))))
                for rset, tiles in segs:
                    tmpt, tmpr = xdring.next()
                    self.fill_mod(mod, layer, rset, 1, self.I['norm2_g'], tmpt, tmpr)
                    blocks = [tiles[i:i + 4] for i in range(0, len(tiles), 4)]
                    nb = len(blocks)
                    bufs = [(bufi[0] + m) % 2 for m in range(nb)]
                    bufi[0] += nb

                    order = [(0, j) for j in range(len(blocks[0]))]
                    for m_ in range(nb):
                        if m_ + 1 < nb:
                            n1_ = len(blocks[m_ + 1])
                            if m_ == 0:
                                order.append((1, 0))
                            order += [(m_ + 1, j) for j in range(1, n1_ - 1)]
                            if n1_ > 1:
                                order.append((m_ + 1, n1_ - 1))
                            if m_ + 2 < nb:
                                order.append((m_ + 2, 0))
                    pos = {mj: i for i, mj in enumerate(order)}

                    def prep_tile(m, j):
                        blk = blocks[m]
                        left_to = bufs[m + 1] if (j == len(blk) - 1 and m + 1 < nb) else None
                        right_to = (bufs[m - 1], 1 + len(blocks[m - 1]) * 128) if (j == 0 and m > 0) else None
                        hb, hbr = prepA(s, blk[j])
                        i = pos[(m, j)]
                        if i + 1 < len(order):
                            m2, j2 = order[i + 1]
                            prefetch(s, blocks[m2][j2])
                        yield
                        prepB(hb, hbr, bufs[m], j, left_to, right_to)
                        yield

                    def preps(lst):
                        for (m_, j_) in lst:
                            yield from prep_tile(m_, j_)

                    def run(g):
                        for _ in g:
                            pass

                    S.memset('vector', hT[bufs[0]][:, :, 0:1], 0.0, writes=[hTL[bufs[0]]])
                    for j in range(len(blocks[0])):
                        run(prep_tile(0, j))
                    if nb > 1:
                        run(prep_tile(1, 0))
                    for m in range(nb):
                        nt = len(blocks[m])
                        if m + 1 >= nb:
                            S.memset('vector', hT[bufs[m]][:, :, 1 + nt * 128:2 + nt * 128], 0.0, writes=[hTR[bufs[m]]])
                        gens = [up(bufs[m], nt)]
                        if m + 1 < nb:
                            n1 = len(blocks[m + 1])
                            gens.append(preps([(m + 1, j) for j in range(1, n1 - 1)]))
                        for _ in ileave(gens, [3, 1]):
                            pass
                        gens = [down(s, blocks[m], last)]
                        later = []
                        if m + 1 < nb and len(blocks[m + 1]) > 1:
                            later.append((m + 1, len(blocks[m + 1]) - 1))
                        if m + 2 < nb:
                            later.append((m + 2, 0))
                        if later:
                            gens.append(preps(later))
                        for _ in ileave(gens, [1, 1]):
                            pass
            S.flush()

    def phase_attn(self, layer, j, last):
        cfg, S, nc = self.cfg, self.S, self.nc
        NTC, NTX, NT = cfg.NTC, cfg.NTX, cfg.NT
        with ExitStack() as st:
            self.common_consts(st)
            mprev, mprevr = self.load_const(st, 'mprev', BF16)
            mnext, mnextr = self.load_const(st, 'mnext', BF16)
            win = self.alloc(st, 'win', [128, KC, 1536], BF16)
            winr = S.res('win')
            wsrc = self.I['attn_w_in'][j]
            qk_src = wsrc[:, 0:1280].rearrange("(kc p) (h a b i) -> p kc h a b i", p=128, a=2, b=2, i=16)
            qk_dst = win[:, :, 0:1280].rearrange("p kc (h b a i) -> p kc h b a i", b=2, a=2, i=16)
            v_src = wsrc[:, 1280:1536].rearrange("(kc p) n -> p kc n", p=128)
            for kc in range(KC):
                for a in range(2):
                    for b in range(2):
                        S.dma('gpsimd', qk_dst[:, kc, :, b, a, :], qk_src[:, kc, :, a, b, :], writes=[winr], accum=True)
            S.dma('gpsimd', win[:, :, 1280:1536], v_src, writes=[winr], accum=True)
            wout = self.alloc(st, 'wout', [128, KC, D], BF16)
            woutr = S.res('wout')
            S.dma('gpsimd', wout[:], self.I['attn_w_out'][j].rearrange("(kc p) n -> p kc n", p=128), writes=[woutr])
            g64 = self.alloc(st, 'g64', [128, 2, 64], F32)
            g64r = S.res('g64')
            for qi, key in enumerate(('attn_q_gain', 'attn_k_gain')):
                gsrc = self.I[key][j].rearrange("(a b i) -> a b i", a=2, b=2)
                gdst = g64[:, qi, :].rearrange("p (b a i) -> p b a i", b=2, a=2)
                for a in range(2):
                    for b in range(2):
                        S.dma('sync', gdst[:, b, a, :], gsrc[a, b, :].partition_broadcast(128), writes=[g64r], accum=True)
            GN = self.alloc(st, 'GN', [128, 1280], F32)
            GNr = S.res('GN')
            S.ts('vector', GN[:, 0:1024].rearrange("p (h f) -> p h f", f=64),
                 g64[:, 0, :].unsqueeze(1).to_broadcast([128, 16, 64]), 0.125, None, ALU.mult, reads=[g64r], writes=[GNr])
            S.cp('vector', GN[:, 1024:1280].rearrange("p (h f) -> p h f", f=64),
                 g64[:, 1, :].unsqueeze(1).to_broadcast([128, 4, 64]), reads=[g64r], writes=[GNr])
            esink = self.alloc(st, 'esink', [128, 16], F32)
            esr = S.res('esink')
            S.dma('sync', esink[:], self.I['attn_sink'][j].partition_broadcast(128), writes=[esr])
            S.act(esink[:], esink[:], AF.Exp, reads=[esr], writes=[esr])
            cosT = self.alloc(st, 'cosT', [128, NTX, 32], F32)
            sinT = self.alloc(st, 'sinT', [128, NTX, 32], F32)
            ropr = S.res('rope')
            S.dma('sync', cosT[:], self.C['rcos'].rearrange("(t p) f -> p t f", p=128), writes=[ropr], accum=True)
            S.dma('sync', sinT[:], self.C['rsin'].rearrange("(t p) f -> p t f", p=128), writes=[ropr], accum=True)
            modC = self.load_mod(st, layer, 2, 0, 'amC', None)
            modS = self.load_mod(st, layer, 0, 0, 'amS', None)
            KT = self.alloc(st, 'KT', [128, 2, cfg.TOK], BF16)
            KTr = [S.res('KT%d' % t) for t in range(NT)]
            VA = self.alloc(st, 'VA', [128, NT, 4, 66], BF16)
            VAr = [S.res('VA%d' % t) for t in range(NT)]
            VA1 = S.res('VAones')
            xring = self.ring(st, 'ax', 2, [128, D], F32)
            xaring = self.ring(st, 'axa', 1, [128, D], F32)
            pre_x = {}
            hbring = self.ring(st, 'ahb', 2, [128, D], BF16)
            string = self.ring(st, 'ast', 4, [128, 4], F32)
            hTring = self.ring(st, 'ahT', 2, [128, KC, 128], BF16)
            sqring = self.ring(st, 'asq', 1, [128, 1280], F32)
            qnring = self.ring(st, 'aqn', 1, [128, 1280], F32)
            Bring = self.ring(st, 'aB', 1, [128, 1280], F32)
            qkbring = self.ring(st, 'aqkb', 2, [128, 1280], BF16)
            ssring = self.ring(st, 'ass', 2, [128, 40], F32)
            QTring = self.ring(st, 'aQT', 4, [128, 8, 128], BF16)
            PTring = self.ring(st, 'aPT', 4, [128, 5, 512], BF16)
            obring = self.ring(st, 'aob', 2, [128, D], BF16)
            oTring = self.ring(st, 'aoT', 1, [128, KC, 128], BF16)
            dnring = self.ring(st, 'adn', 4, [128, 8], F32)
            tmpring = self.ring(st, 'atmp', 1, [128, 512], F32)
            ptr = self.ring(st, 'aptr', 1, [128, KC, 128], BF16, psum=True)
            pqkv = self.ring(st, 'apq', 1, [128, 1536], F32, psum=True)
            pO = self.ring(st, 'apO', 2, [128, 4, 66], F32, psum=True)
            qt_of = {}

            pSY = self.alloc_psum_pair(st)

            def prefetch(s, t):
                xt, xr = xring.next()
                S.dma('sync', xt[:], self.x_src(s, t), writes=[xr])
                pre_x[(s, t)] = (xt, xr)

            normed = {}

            def do_norm(s, t):
                mod = modC if t < NTC else modS
                if (s, t) not in pre_x:
                    prefetch(s, t)
                xt, xr = pre_x.pop((s, t))
                hb, hbr = hbring.next()
                stt_, str_ = string.next()
                self.norm_tile(xt, xr, mod, stt_, str_, hb, hbr)
                normed[(s, t)] = (hb, hbr)
                if t + 1 < NT:
                    prefetch(s, t + 1)

            def proj(s, t):
                if (s, t) not in normed:
                    do_norm(s, t)
                hb, hbr = normed.pop((s, t))
                yield
                pt, pr = ptr.next()
                for kc in range(KC):
                    S.tr(pt[:, kc, :], hb[:, kc * 128:(kc + 1) * 128], self.ident[:], reads=[hbr, self.identr], writes=[pr])
                hT, hTr = hTring.next()
                S.cp('scalar', hT[:], pt[:], reads=[pr], writes=[hTr])
                yield
                pq, pqr = pqkv.next()
                for nb in range(3):
                    for kc in range(KC):
                        S.mm(pq[:, nb * 512:(nb + 1) * 512], hT[:, kc, :], win[:, kc, nb * 512:(nb + 1) * 512],
                             kc == 0, kc == KC - 1, reads=[hTr, winr], writes=[pqr])
                    yield
                sq, sqr = sqring.next()
                S.act(sq[:], pq[:, 0:1280], AF.Square, reads=[pqr], writes=[sqr])
                S.cp('scalar', VA[:, t, :, 0:64], pq[:, 1280:1536].rearrange("p (h f) -> p h f", f=64), reads=[pqr],
                     writes=[VAr[t]])
                yield
                ss, ssr = ssring.next()
                S.red(ss[:, 0:20], sq[:].rearrange("p (h f) -> p h f", f=64), ALU.add, reads=[sqr], writes=[ssr])
                S.act(ss[:, 20:40], ss[:, 0:20], AF.Ln, reads=[ssr, self.epsr], writes=[ssr], scale=1.0 / 64, bias=self.epsb[:, 0:1])
                S.act(ss[:, 20:40], ss[:, 20:40], AF.Exp, reads=[ssr], writes=[ssr], scale=-0.5)
                yield
                qn, qnr = qnring.next()
                S.tt('vector', qn[:].rearrange("p (h f) -> p h f", f=64), pq[:, 0:1280].rearrange("p (h f) -> p h f", f=64),
                     ss[:, 20:40].unsqueeze(2).to_broadcast([128, 20, 64]), ALU.mult, reads=[pqr, ssr], writes=[qnr])
                yield
                S.tt('vector', qn[:], qn[:], GN[:], ALU.mult, reads=[qnr, GNr], writes=[qnr])
                yield
                qkb, qkbr = qkbring.next()
                qo = qkb[:, 0:1024].rearrange("p (h a b f) -> p a h b f", a=2, b=2, f=32)
                ko = qkb[:, 1024:1280].rearrange("p (k a b f) -> p a k b f", a=2, b=2, f=32)
                v4 = lambda ap: ap.rearrange("p (h b f) -> p h b f", b=2, f=32)
                vq = lambda ap: v4(ap)[:, 0:16].rearrange("p (a h) b f -> p a h b f", a=2)
                vk = lambda ap: v4(ap)[:, 16:20].rearrange("p (a k) b f -> p a k b f", a=2)
                if t >= NTC:
                    tt_ = t - NTC
                    Bt, Br = Bring.next()
                    cb_ = cosT[:, tt_, :].unsqueeze(1).unsqueeze(1).to_broadcast([128, 20, 2, 32])
                    sb_ = sinT[:, tt_, :].unsqueeze(1).unsqueeze(1).to_broadcast([128, 20, 2, 32])
                    S.tt('vector', v4(sq[:]), v4(qn[:]), cb_, ALU.mult, reads=[qnr, ropr], writes=[sqr])
                    S.tt('vector', v4(Bt[:]), v4(qn[:]), sb_, ALU.mult, reads=[qnr, ropr], writes=[Br])
                    yield
                    for vv, oo, eng_ in ((vq, qo, 'vector'), (vk, ko, 'gpsimd')):
                        S.tt(eng_, oo[:, :, :, 0, :], vv(sq[:])[:, :, :, 0, :], vv(Bt[:])[:, :, :, 1, :], ALU.subtract,
                             reads=[sqr, Br], writes=[qkbr], accum=True)
                        S.tt(eng_, oo[:, :, :, 1, :], vv(Bt[:])[:, :, :, 0, :], vv(sq[:])[:, :, :, 1, :], ALU.add,
                             reads=[sqr, Br], writes=[qkbr], accum=True)
                    yield
                else:
                    S.cp('gpsimd', qo, vq(qn[:]), reads=[qnr], writes=[qkbr], accum=True)
                    S.cp('gpsimd', ko, vk(qn[:]), reads=[qnr], writes=[qkbr], accum=True)
                    yield
                if t + 1 < NT:
                    do_norm(s, t + 1)
                    yield
                QT, QTr = QTring.next()
                qt_of[(s, t)] = (QT, QTr)
                pt, pr = ptr.next()
                for h in range(8):
                    S.tr(pt[:, h, :], qkb[:, h * 128:(h + 1) * 128], self.ident[:], reads=[qkbr, self.identr], writes=[pr])
                S.cp('vector', QT[:], pt[:], reads=[pr], writes=[QTr])
                yield
                pt, pr = ptr.next()
                for k_ in range(2):
                    S.tr(pt[:, k_, :], qkb[:, 1024 + k_ * 128:1024 + (k_ + 1) * 128], self.ident[:],
                         reads=[qkbr, self.identr], writes=[pr])
                S.cp('scalar', KT[:, :, t * 128:(t + 1) * 128], pt[:, 0:2, :], reads=[pr], writes=[KTr[t]])
                yield

            def attend_kv(kv, chunks, QT, QTr, ob, obr):
                nch = len(chunks)
                PT, PTr = PTring.next()
                for ci, (kt, mk, mkr) in enumerate(chunks):
                    sp, spr = pSY[(ci + (kv // 2)) % 2]
                    ph = slice(0, 64) if kv < 2 else slice(64, 128)
                    kk_ = kv % 2
                    S.mm(sp[:], KT[ph, kk_, kt * 128:(kt + 1) * 128], QT[ph, kk_ * 4:(kk_ + 1) * 4, :], True, True,
                         reads=[KTr[kt], QTr], writes=[spr])
                    S.act(PT[:, ci, :], sp[:], AF.Exp, reads=[spr], writes=[PTr], accum=True)
                    if mk is not None:
                        pv_ = PT[:, ci, :].rearrange("p (g q) -> p g q", g=4)
                        S.tt('gpsimd', pv_, pv_, mk[:].unsqueeze(1).to_broadcast([128, 4, 128]), ALU.mult,
                             reads=[PTr, mkr], writes=[PTr])
                    yield
                O, Or = pO.next()
                for g in range(4):
                    for ci, (kt, mk, mkr) in enumerate(chunks):
                        S.mm(O[:, g, 0:65], PT[:, ci, g * 128:(g + 1) * 128], VA[:, kt, kv, 0:65], ci == 0, ci == nch - 1,
                             reads=[PTr, VAr[kt], VA1], writes=[Or])
                dn, dnr = dnring.next()
                S.tt('vector', dn[:, 0:4], O[:, :, 64], esink[:, kv * 4:(kv + 1) * 4], ALU.add, reads=[Or, esr], writes=[dnr])
                S.recip(dn[:, 4:8], dn[:, 0:4], reads=[dnr], writes=[dnr])
                S.tt('vector', ob[:, kv * 256:(kv + 1) * 256].rearrange("p (g f) -> p g f", f=64), O[:, :, 0:64],
                     dn[:, 4:8].unsqueeze(2).to_broadcast([128, 4, 64]), ALU.mult, reads=[Or, dnr], writes=[obr], accum=True)
                yield

            def attend(s, t):
                mod = modC if t < NTC else modS
                QT, QTr = qt_of.pop((s, t))
                if t < NTC:
                    chunks = [(k, None, None) for k in range(NTC)]
                else:
                    chunks = []
                    if t > NTC:
                        chunks.append((t - 1, mprev, mprevr))
                    chunks.append((t, None, None))
                    if t < NT - 1:
                        chunks.append((t + 1, mnext, mnextr))
                    chunks += [(k, None, None) for k in range(NTC)]
                ob, obr = obring.next()
                yield from ileave([attend_kv(kv_, chunks, QT, QTr, ob, obr) for kv_ in (0, 2, 1, 3)])
                pt, pr = ptr.next()
                for kc in range(KC):
                    S.tr(pt[:, kc, :], ob[:, kc * 128:(kc + 1) * 128], self.ident[:], reads=[obr, self.identr], writes=[pr])
                oT, oTr = oTring.next()
                S.cp('scalar', oT[:], pt[:], reads=[pr], writes=[oTr])
                yield
                xt, xr = xaring.next()
                S.dma('sync', xt[:], self.x_src(s, t), writes=[xr])
                G, SH, GA, mr = mod
                for half in range(2):
                    y, yr = pSY[half]
                    for kc in range(KC):
                        S.mm(y[:], oT[:, kc, :], wout[:, kc, half * 512:(half + 1) * 512],
                             kc == 0, kc == KC - 1, reads=[oTr, woutr], writes=[yr])
                    tmp, tmpr = tmpring.next()
                    S.tt('vector', tmp[:], y[:], GA[:, half * 512:(half + 1) * 512], ALU.mult,
                         reads=[yr, mr], writes=[tmpr])
                    S.tt('gpsimd', xt[:, half * 512:(half + 1) * 512], tmp[:], xt[:, half * 512:(half + 1) * 512],
                         ALU.add, reads=[tmpr, xr], writes=[xr])
                S.dma('sync', self.x_dst(s, t), xt[:], reads=[xr])
                self.written.add((s, t))
                yield

            S.memset('vector', VA[:, :, :, 64:66], 1.0, writes=[VA1])
            tmpt, tmpr_ = xaring.next()
            self.fill_mod(modC, layer, 2, 0, self.I['norm1_g'], tmpt, tmpr_)
            for s in range(cfg.NS):
                tmpt, tmpr_ = xaring.next()
                self.fill_mod(modS, layer, s, 0, self.I['norm1_g'], tmpt, tmpr_)
                pend = []
                if not last:
                    pend += [(t, NTC - 1) for t in range(NTC)]
                pend += [(t, min(t + 1, NT - 1)) for t in range(NTC, NT)]
                for k in range(NT):
                    gens = [proj(s, k)]
                    if pend and pend[0][1] < k:
                        gens.append(attend(s, pend.pop(0)[0]))
                    for _ in ileave(gens, [1, 2] if len(gens) == 2 else None):
                        pass
                    if last and k < NTC:
                        qt_of.pop((s, k))
                while pend:
                    for _ in attend(s, pend.pop(0)[0]):
                        pass
            S.flush()

    def alloc_psum_pair(self, st):
        S = self.S
        items = []
        for i in range(2):
            t = self.palloc(st, 'apSY%d' % i, [128, 512], F32)
            items.append((t, S.res('apSY%d' % i)))
        return items

    def phase_lb(self):
        cfg, S = self.cfg, self.S
        dp = cfg.depth
        with ExitStack() as st:
            E = self.alloc(st, 'lbE', [128, dp, D], F32)
            Er = S.res('lbE')
            S.dma('sync', E[:].rearrange("p a b -> p (a b)"),
                  self.I['hgrn_lb_logits'].rearrange("a b -> (a b)").partition_broadcast(128), writes=[Er])
            S.act(E[:], E[:], AF.Exp, reads=[Er], writes=[Er])
            tot = self.alloc(st, 'lbtot', [128, D], F32)
            num = self.alloc(st, 'lbnum', [128, D], F32)
            lb = self.alloc(st, 'lblb', [128, 2, D], F32)
            tr_, nr_, lr_ = S.res('tot'), S.res('num'), S.res('lb')
            S.cp('vector', tot[:], E[:, 0, :], reads=[Er], writes=[tr_])
            for j in range(1, dp):
                S.tt('vector', tot[:], tot[:], E[:, j, :], ALU.add, reads=[Er, tr_], writes=[tr_])
            S.recip(tot[:], tot[:], reads=[tr_], writes=[tr_])
            for li in range(cfg.nh):
                layer = 2 * li + 1
                S.cp('vector', num[:], E[:, 1, :], reads=[Er], writes=[nr_])
                for j in range(2, layer + 1):
                    S.tt('vector', num[:], num[:], E[:, j, :], ALU.add, reads=[Er, nr_], writes=[nr_])
                S.tt('vector', lb[:, 0, :], num[:], tot[:], ALU.mult, reads=[nr_, tr_], writes=[lr_])
                S.ts('vector', lb[:, 1, :], lb[:, 0, :], -1.0, 1.0, ALU.mult, ALU.add, reads=[lr_], writes=[lr_])
                S.dma('sync', self.lbv[li:li + 1], lb[0:1, :, :], reads=[lr_], writes=[lr_])
            S.flush()

    def chain_gen(self, hg, bwd, QeT, QeTz, KinvT, Kdec, vb, Dd, rd, mask, maskr, Sst, Sr, SbA, SbAr, SbB, SbBr,
                  ATring, pA, pO, pD, need_o, evac):
        S = self.S
        hs = slice(hg * 4, (hg + 1) * 4)
        first, second = (1, 0) if bwd else (0, 1)
        S.cp('scalar', SbA[:, hs, :], Sst[:, hs, :], reads=[Sr], writes=[SbAr], accum=True)
        yield
        for idx, sub in enumerate((first, second)):
            rows = slice(sub * 64, (sub + 1) * 64)
            d, dr = pD.next()
            for h in range(4):
                hd = hg * 4 + h
                S.mm(d[:, h, :], Kdec[rows, hd * 128:(hd + 1) * 128], vb[rows, hd * 128:(hd + 1) * 128], True, True,
                     reads=rd, writes=[dr])
            S.tt('vector', Sst[:, hs, :], Sst[:, hs, :], Dd[:, hs, sub:sub + 1].to_broadcast([128, 4, 128]), ALU.mult,
                 reads=[Sr, SbAr, SbBr] + rd, writes=[Sr])
            S.tt('vector', Sst[:, hs, :], Sst[:, hs, :], d[:], ALU.add, reads=[Sr, dr], writes=[Sr])
            yield
            if idx == 0:
                S.cp('scalar', SbB[:, hs, :], Sst[:, hs, :], reads=[Sr], writes=[SbBr], accum=True)
                yield
        if need_o:
            a, ar = pA.next()
            for h in range(4):
                hd = hg * 4 + h
                S.mm(a[:, h, :], KinvT[:, hd, :], QeT[:, hd, :], True, True, reads=rd, writes=[ar])
            atm, atmr = ATring.next()
            S.tt('vector', atm[:], a[:], mask[:].unsqueeze(1).to_broadcast([128, 4, 128]), ALU.mult, reads=[ar, maskr],
                 writes=[atmr])
            yield
            o, orr = pO.next()
            S_lo, S_lor = (SbB, SbBr) if bwd else (SbA, SbAr)
            S_hi, S_hir = (SbA, SbAr) if bwd else (SbB, SbBr)
            for h in range(4):
                hd = hg * 4 + h
                S.mm(o[:, h, :], atm[:, h, :], vb[:, hd * 128:(hd + 1) * 128], True, False, reads=[atmr] + rd, writes=[orr])
                S.mm(o[0:64, h, :], QeT[:, hd, 0:64], S_lo[:, hd, :], False, False, reads=rd + [S_lor], writes=[orr])
                S.mm(o[:, h, :], QeTz[:, hd, :], S_hi[:, hd, :], False, True, reads=rd + [S_hir], writes=[orr])
            evac(hg, o, orr)
            yield

    def phase_hgrn(self, layer, j, last):
        self.phase_hf(layer, j, last)
        self.phase_hb(layer, j, last)

    def phase_hf(self, layer, j, last):
        cfg, S, nc = self.cfg, self.S, self.nc
        NTC, NT = cfg.NTC, cfg.NT
        with ExitStack() as st:
            self.common_consts(st)
            maskf, maskfr = self.load_const(st, 'm1f', BF16)
            m1 = [self.load_const(st, k, F32) for k in ('m1f', 'm1b')]
            m2 = [self.load_const(st, k, F32) for k in ('m2f', 'm2b')]
            c2 = [self.load_const(st, k, F32) for k in ('c2f', 'c2b')]
            win = self.alloc(st, 'hwin', [128, KC, 5120], BF16)
            winr = S.res('hwin')
            wsrc = self.I['hgrn_w_in'][j].rearrange("(kc p) n -> p kc n", p=128)
            for kc in range(KC):
                S.dma('gpsimd', win[:, kc, :], wsrc[:, kc, :], writes=[winr], accum=True)
            LBt = self.alloc(st, 'LBt', [128, 2, D], F32)
            LBr = S.res('LBt')
            S.dma('sync', LBt[:].rearrange("p a b -> p (a b)"),
                  self.lbv[j].rearrange("a b -> (a b)").partition_broadcast(128), writes=[LBr])
            modC = self.load_mod(st, layer, 2, 0, 'hmC', None, need='GS')
            modS = self.load_mod(st, layer, 0, 0, 'hmS', None, need='GS')
            Sst = self.alloc(st, 'Sst', [128, 8, 128], F32)
            Sr = S.res('Sst')
            SbA = self.alloc(st, 'SbA', [128, 8, 128], BF16)
            SbAr = S.res('SbA')
            SbB = self.alloc(st, 'SbB', [128, 8, 128], BF16)
            SbBr = S.res('SbB')
            qzring = self.ring(st, 'hqz', 2, [128, KC, 128], BF16)
            for qz_, qzr_ in qzring.items:
                S.memset('gpsimd', qz_[:], 0.0, writes=[qzr_])
            xring = self.ring(st, 'hx', 2, [128, D], F32)
            hbring = self.ring(st, 'hhb', 2, [128, D], BF16)
            string = self.ring(st, 'hst', 4, [128, 4], F32)
            hTring = self.ring(st, 'hhT', 2, [128, KC, 128], BF16)
            sqring = self.ring(st, 'hsq', 1, [128, D], F32)
            vbring = self.ring(st, 'hvb', 2, [128, D], BF16)
            sgring = self.ring(st, 'hsgt', 2, [128, D], BF16)
            fring = self.ring(st, 'hf', 4, [128, 512], F32)
            kkring = self.ring(st, 'hkk', 4, [128, 512], F32)
            exring = self.ring(st, 'hex', 4, [128, 512], F32)
            qering = [self.ring(st, 'hqe%d' % d, 1, [128, D], BF16) for d in range(2)]
            kiring = [self.ring(st, 'hki%d' % d, 1, [128, D], BF16) for d in range(2)]
            kdring = [self.ring(st, 'hkd%d' % d, 2 - d, [128, D], BF16) for d in range(2)]
            qeTring = [self.ring(st, 'hqeT%d' % d, 2 - d, [128, KC, 128], BF16) for d in range(2)]
            kiTring = [self.ring(st, 'hkiT%d' % d, 2 - d, [128, KC, 128], BF16) for d in range(2)]
            ddring = [self.ring(st, 'hdd%d' % d, 2, [128, 8, 2], F32) for d in range(2)]
            ATring = self.ring(st, 'hAT', 2, [128, 4, 128], BF16)
            ofring = self.ring(st, 'hof', 2, [128, 512], F32)
            ptr = self.ring(st, 'hptr', 1, [128, KC, 128], BF16, psum=True)
            pp = self.ring(st, 'hpp', 2, [128, 512], F32, psum=True)
            pE = self.ring(st, 'hpE', 2, [128, 512], F32, psum=True)
            pA = self.ring(st, 'hpA', 1, [128, 4, 128], F32, psum=True)
            pO = self.ring(st, 'hpO', 1, [128, 4, 128], F32, psum=True)
            pD = self.ring(st, 'hpD', 1, [128, 4, 128], F32, psum=True)

            def project(hT, hTr, col0):
                p, pr = pp.next()
                for kc in range(KC):
                    S.mm(p[:], hT[:, kc, :], win[:, kc, col0:col0 + 512], kc == 0, kc == KC - 1, reads=[hTr, winr], writes=[pr])
                return p, pr

            def sub_chain(d, half, hT, hTr, sq, sqr, qe, qer, ki, kir, kd, kdr, dd, ddr):
                hs = slice(half * 512, (half + 1) * 512)
                p, pr = project(hT, hTr, 1024 * (1 + d) + half * 512)
                f, fr = fring.next()
                S.act(f[:], p[:], AF.Sigmoid, reads=[pr], writes=[fr])
                yield
                fe = 'vector' if (d == 0 or half == 0) else 'gpsimd'
                S.tt(fe, f[:], f[:], LBt[:, 1, hs], ALU.mult, reads=[fr, LBr], writes=[fr])
                S.tt(fe, f[:], f[:], LBt[:, 0, hs], ALU.add, reads=[fr, LBr], writes=[fr])
                yield
                kk, kkr = kkring.next()
                S.ts('gpsimd', kk[:], f[:], -1.0, 1.0, ALU.mult, ALU.add, reads=[fr], writes=[kkr])
                S.act(f[:], f[:], AF.Ln, reads=[fr, kkr], writes=[fr])
                yield
                e1, e1r = pE.next()
                S.mm(e1[:], m1[d][0][:], f[:], True, True, reads=[m1[d][1], fr], writes=[e1r])
                ex, exr = exring.next()
                S.act(ex[:], e1[:], AF.Exp, reads=[e1r], writes=[exr])
                S.tt('vector', qe[:, hs], sq[:, hs], ex[:], ALU.mult, reads=[sqr, exr], writes=[qer], accum=True)
                ex, exr = exring.next()
                S.act(ex[:], e1[:], AF.Exp, reads=[e1r], writes=[exr], scale=-1.0)
                S.tt('gpsimd', ki[:, hs], kk[:], ex[:], ALU.mult, reads=[kkr, exr], writes=[kir], accum=True)
                yield
                e3, e3r = pE.next()
                S.mm(e3[:], m2[d][0][:], f[:], True, True, reads=[m2[d][1], fr], writes=[e3r])
                ex, exr = exring.next()
                S.act(ex[:], e3[:], AF.Exp, reads=[e3r], writes=[exr])
                S.tt('vector', kd[:, hs], kk[:], ex[:], ALU.mult, reads=[kkr, exr], writes=[kdr], accum=True)
                yield
                pd_, pdr = pE.next()
                for h in range(4):
                    S.mm(pd_[:, h * 2:h * 2 + 2], f[:, h * 128:(h + 1) * 128], c2[d][0][:], True, True,
                         reads=[fr, c2[d][1]], writes=[pdr])
                S.act(dd[:, half * 4:(half + 1) * 4, :], pd_[:, 0:8].rearrange("p (h c) -> p h c", c=2), AF.Exp,
                      reads=[pdr], writes=[ddr], accum=True)
                yield

            def load_x(s, t):
                xt, xr = xring.next()
                S.dma('sync', xt[:], self.x_src(s, t), writes=[xr])
                return xt, xr

            normed = {}

            def do_norm(s, t):
                mod = modC if t < NTC else modS
                if (s, t) not in pre_x:
                    pre_x[(s, t)] = load_x(s, t)
                xt, xr = pre_x.pop((s, t))
                hb, hbr = hbring.next()
                stt_, str_ = string.next()
                self.norm_tile(xt, xr, mod, stt_, str_, hb, hbr)
                normed[(s, t)] = (hb, hbr)
                i_ = tpos[(s, t)]
                if i_ + 1 < len(tiles) and tiles[i_ + 1][1] != 0:
                    pre_x[tiles[i_ + 1]] = load_x(*tiles[i_ + 1])

            def prep(s, t, nxt, res):
                if (s, t) not in normed:
                    do_norm(s, t)
                hb, hbr = normed.pop((s, t))
                yield
                pt, pr = ptr.next()
                for kc in range(KC):
                    S.tr(pt[:, kc, :], hb[:, kc * 128:(kc + 1) * 128], self.ident[:], reads=[hbr, self.identr], writes=[pr])
                hT, hTr = hTring.next()
                S.cp('scalar', hT[:], pt[:], reads=[pr], writes=[hTr])
                yield
                sq, sqr = sqring.next()
                vb, vbr = vbring.next()
                sg, sgr = sgring.next()
                bufs = []
                gens = []
                for d in range(2):
                    qe, qer = qering[d].next()
                    ki, kir = kiring[d].next()
                    kd, kdr = kdring[d].next()
                    dd, ddr = ddring[d].next()
                    bufs.append((qe, qer, ki, kir, kd, kdr, dd, ddr))
                    for half in range(2):
                        gens.append(sub_chain(d, half, hT, hTr, sq, sqr, qe, qer, ki, kir, kd, kdr, dd, ddr))
                for g_ in gens:
                    next(g_)
                    yield
                for half in range(2):
                    hs = slice(half * 512, (half + 1) * 512)
                    p, pr = project(hT, hTr, half * 512)
                    S.act(sq[:, hs], p[:], AF.Silu, reads=[pr], writes=[sqr], accum=True)
                    next(gens[2 * half])
                    yield
                    p, pr = project(hT, hTr, 4096 + half * 512)
                    S.act(sg[:, hs], p[:], AF.Silu, reads=[pr], writes=[sgr], accum=True)
                    next(gens[2 * half + 1])
                    yield
                    p, pr = project(hT, hTr, 3072 + half * 512)
                    S.cp('vector', vb[:, hs], p[:], reads=[pr], writes=[vbr], accum=True)
                    yield
                S.dma('sync', self.hv[s, t], vb[:], reads=[vbr])
                S.dma('sync', self.hsg[s, t], sg[:], reads=[sgr])
                yield from ileave(gens)
                if nxt is not None and nxt[1] != 0:
                    do_norm(*nxt)
                    yield
                ops = []
                for d in range(2):
                    qe, qer, ki, kir, kd, kdr, dd, ddr = bufs[d]
                    qeT, qeTr = qeTring[d].next()
                    kiT, kiTr = kiTring[d].next()
                    qz, qzr = (None, None)
                    for src, srcr, dst, dstr in ((qe, qer, qeT, qeTr), (ki, kir, kiT, kiTr)):
                        pt, pr = ptr.next()
                        for kc in range(KC):
                            S.tr(pt[:, kc, :], src[:, kc * 128:(kc + 1) * 128], self.ident[:], reads=[srcr, self.identr],
                                 writes=[pr])
                        S.cp('scalar', dst[:], pt[:], reads=[pr], writes=[dstr])
                        if d == 0 and src is qe:
                            qz, qzr = qzring.next()
                            S.cp('gpsimd', qz[:, :, 64:128], dst[:, :, 64:128], reads=[dstr], writes=[qzr])
                        yield
                    ops.append((qeT, qeTr, kiT, kiTr, kd, kdr, dd, ddr, qz, qzr))
                qeT, qeTr, kiT, kiTr, kd, kdr, dd, ddr, _, _ = ops[1]
                S.dma('sync', self.hqb[s, t], qeT[:], reads=[qeTr])
                S.dma('sync', self.hkb[s, t], kiT[:], reads=[kiTr])
                S.dma('sync', self.hkd[s, t], kd[:], reads=[kdr])
                S.dma('sync', self.hdb[s, t], dd[:].rearrange("p h c -> p (h c)"), reads=[ddr])
                res['ops'] = ops[0] + (vb, vbr)

            def chain(s, t, res):
                qeT, qeTr, kiT, kiTr, kd, kdr, dd, ddr, qz, qzr, vb, vbr = res['ops']
                rd = [qeTr, kiTr, kdr, ddr, vbr, qzr]
                if t == 0:
                    S.memset('vector', Sst[:], 0.0, writes=[Sr])
                need_o = not (last and t < NTC)

                def evac(hg, o, orr):
                    of, ofr = ofring.next()
                    S.cp('scalar', of[:], o[:].rearrange("p h v -> p (h v)"), reads=[orr], writes=[ofr])
                    S.dma('sync', self.hof[s, t, :, hg * 512:(hg + 1) * 512], of[:], reads=[ofr])

                gens = [self.chain_gen(hg, False, qeT, qz, kiT, kd, vb, dd, rd, maskf, maskfr, Sst, Sr, SbA, SbAr, SbB, SbBr,
                                       ATring, pA, pO, pD, need_o, evac) for hg in range(2)]
                yield from ileave(gens)

            tmpt, tmpr_ = xring.next()
            self.fill_mod(modC, layer, 2, 0, self.I['norm1_g'], tmpt, tmpr_)
            tiles = [(s, t) for s in range(cfg.NS) for t in range(NT)]
            prev = None
            tpos = {st_: i for i, st_ in enumerate(tiles)}
            pre_x = {}
            for i, (s, t) in enumerate(tiles):
                if t == 0:
                    tmpt, tmpr_ = xring.next()
                    self.fill_mod(modS, layer, s, 0, self.I['norm1_g'], tmpt, tmpr_)
                nxt = tiles[i + 1] if i + 1 < len(tiles) else None
                res = {}
                gens = [prep(s, t, nxt, res)]
                if prev is not None:
                    gens.append(chain(prev[0], prev[1], prev[2]))
                if os.environ.get('HF_SEQ', '0') == '1':
                    for g in reversed(gens):
                        for _ in g:
                            pass
                else:
                    for _ in ileave(gens, [2, 1]):
                        pass
                prev = (s, t, res)
            for _ in chain(prev[0], prev[1], prev[2]):
                pass
            S.flush()

    def phase_hb(self, layer, j, last):
        cfg, S, nc = self.cfg, self.S, self.nc
        NTC, NT = cfg.NTC, cfg.NT
        with ExitStack() as st:
            self.common_consts(st)
            maskb, maskbr = self.load_const(st, 'm1b', BF16)
            wout = self.alloc(st, 'hwout', [128, KC, D], BF16)
            woutr = S.res('hwout')
            S.dma('gpsimd', wout[:], self.I['hgrn_w_out'][j].rearrange("(kc p) n -> p kc n", p=128), writes=[woutr])
            OG = self.alloc(st, 'OG', [128, 128], F32)
            OGr = S.res('OG')
            S.dma('sync', OG[:, 0:1], self.I['hgrn_o_gain'][j].rearrange("(p o) -> p o", o=1), writes=[OGr])
            S.op('vector', lambda e: e.tensor_scalar(out=wout[:], in0=wout[:], scalar1=OG[:, 0:1], scalar2=None, op0=ALU.mult),
                 reads=[woutr, OGr], writes=[woutr])
            modC = self.load_mod(st, layer, 2, 0, 'bmC', None, need='A')
            modSs = [self.load_mod(st, layer, s, 0, 'bmS%d' % s, None, need='A') for s in range(cfg.NS)]
            states = []
            for s in range(cfg.NS):
                Sst = self.alloc(st, 'bSst%d' % s, [128, 8, 128], F32)
                SbA = self.alloc(st, 'bSbA%d' % s, [128, 8, 128], BF16)
                SbB = self.alloc(st, 'bSbB%d' % s, [128, 8, 128], BF16)
                states.append((Sst, S.res('Sst'), SbA, S.res('SbA'), SbB, S.res('SbB')))
            RS = []
            for s_ in range(cfg.NS):
                R = {}
                R['qz'] = self.ring(st, 'bqz%d' % s_, 2, [128, KC, 128], BF16)
                for qz_, qzr_ in R['qz'].items:
                    S.memset('gpsimd', qz_[:], 0.0, writes=[qzr_])
                R['x'] = self.ring(st, 'bx%d' % s_, 3, [128, D], F32)
                R['q'] = self.ring(st, 'bq%d' % s_, 2, [128, KC, 128], BF16)
                R['k'] = self.ring(st, 'bk%d' % s_, 2, [128, KC, 128], BF16)
                R['kd'] = self.ring(st, 'bkd%d' % s_, 2, [128, D], BF16)
                R['v'] = self.ring(st, 'bv%d' % s_, 2, [128, D], BF16)
                R['sg'] = self.ring(st, 'bsg%d' % s_, 3, [128, D], BF16)
                R['dd'] = self.ring(st, 'bdd%d' % s_, 2, [128, 8, 2], F32)
                R['of'] = self.ring(st, 'bof%d' % s_, 3, [128, D], F32)
                R['o'] = self.ring(st, 'bo%d' % s_, 2, [128, D], F32)
                R['sq'] = self.ring(st, 'bsq%d' % s_, 1, [128, D], F32)
                R['ss'] = self.ring(st, 'bss%d' % s_, 2, [128, 16], F32)
                R['ob'] = self.ring(st, 'bob%d' % s_, 2, [128, D], BF16)
                R['oT'] = self.ring(st, 'boT%d' % s_, 2, [128, KC, 128], BF16)
                R['AT'] = self.ring(st, 'bAT%d' % s_, 2, [128, 4, 128], BF16)
                R['tmp'] = self.ring(st, 'btmp%d' % s_, 2, [128, 512], F32)
                RS.append(R)
            ptr = self.ring(st, 'bptr', 1, [128, KC, 128], BF16, psum=True)
            pA = self.ring(st, 'bpA', 2, [128, 4, 128], F32, psum=True)
            pO = self.ring(st, 'bpO', 2, [128, 4, 128], F32, psum=True)
            pD = self.ring(st, 'bpD', 1, [128, 4, 128], F32, psum=True)
            py = self.ring(st, 'bpy', 1, [128, D], F32, psum=True)

            def loads(s, t):
                need_o = not (last and t < NTC)
                L = {}
                L['q'] = RS[s]['q'].next()
                L['k'] = RS[s]['k'].next()
                L['kd'] = RS[s]['kd'].next()
                L['v'] = RS[s]['v'].next()
                L['dd'] = RS[s]['dd'].next()
                S.dma('sync', L['q'][0][:], self.hqb[s, t], writes=[L['q'][1]])
                S.dma('sync', L['k'][0][:], self.hkb[s, t], writes=[L['k'][1]])
                S.dma('sync', L['kd'][0][:], self.hkd[s, t], writes=[L['kd'][1]])
                S.dma('sync', L['v'][0][:], self.hv[s, t], writes=[L['v'][1]])
                S.dma('sync', L['dd'][0][:].rearrange("p h c -> p (h c)"), self.hdb[s, t], writes=[L['dd'][1]])
                if need_o:
                    L['of'] = RS[s]['of'].next()
                    S.dma('sync', L['of'][0][:], self.hof[s, t], writes=[L['of'][1]])
                    L['sg'] = RS[s]['sg'].next()
                    S.dma('sync', L['sg'][0][:], self.hsg[s, t], writes=[L['sg'][1]])
                    L['x'] = RS[s]['x'].next()
                    S.dma('sync', L['x'][0][:], self.x_src(s, t), writes=[L['x'][1]])
                return L

            def chain_part(s, t, L):
                Sst, Sr, SbA, SbAr, SbB, SbBr = states[s]
                qT, qTr = L['q']
                kT, kTr = L['k']
                kd, kdr = L['kd']
                vb, vbr = L['v']
                dd, ddr = L['dd']
                qz, qzr = RS[s]['qz'].next()
                S.cp('gpsimd', qz[:, :, 64:128], qT[:, :, 64:128], reads=[qTr], writes=[qzr])
                yield
                rd = [qTr, kTr, kdr, vbr, ddr, qzr]
                need_o = not (last and t < NTC)
                o_sb, o_sbr = RS[s]['o'].next() if need_o else (None, None)
                of, ofr = L['of'] if need_o else (None, None)

                def evac(hg, o, orr):
                    hs = slice(hg * 512, (hg + 1) * 512)
                    S.tt('vector', o_sb[:, hs], o[:].rearrange("p h v -> p (h v)"), of[:, hs], ALU.add,
                         reads=[orr, ofr], writes=[o_sbr], accum=True)

                gens = [self.chain_gen(hg, True, qT, qz, kT, kd, vb, dd, rd, maskb, maskbr, Sst, Sr, SbA, SbAr, SbB, SbBr,
                                       RS[s]['AT'], pA, pO, pD, need_o, evac) for hg in range(2)]
                yield from ileave(gens)
                L['o_sb'] = (o_sb, o_sbr)

            def readout_part(s, t, L):
                mod = modC if t < NTC else modSs[s]
                need_o = not (last and t < NTC)
                if not need_o:
                    return
                o_sb, o_sbr = L['o_sb']
                sg, sgr = L['sg']
                sq, sqr = RS[s]['sq'].next()
                S.act(sq[:], o_sb[:], AF.Square, reads=[o_sbr], writes=[sqr])
                yield
                ss, ssr = RS[s]['ss'].next()
                S.red(ss[:, 0:8], sq[:].rearrange("p (h v) -> p h v", v=128), ALU.add, reads=[sqr], writes=[ssr])
                S.act(ss[:, 8:16], ss[:, 0:8], AF.Ln, reads=[ssr, self.epsr], writes=[ssr], scale=1.0 / 128, bias=self.epsb[:, 0:1])
                S.act(ss[:, 8:16], ss[:, 8:16], AF.Exp, reads=[ssr], writes=[ssr], scale=-0.5)
                yield
                ob, obr = RS[s]['ob'].next()
                for h in range(8):
                    hs_ = slice(h * 128, (h + 1) * 128)
                    S.stt('vector', ob[:, hs_], o_sb[:, hs_], ss[:, 8 + h:9 + h], sg[:, hs_], ALU.mult, ALU.mult,
                          reads=[o_sbr, ssr, sgr], writes=[obr], accum=True)
                yield
                pt, pr = ptr.next()
                for kc in range(KC):
                    S.tr(pt[:, kc, :], ob[:, kc * 128:(kc + 1) * 128], self.ident[:], reads=[obr, self.identr], writes=[pr])
                oT, oTr = RS[s]['oT'].next()
                S.cp('scalar', oT[:], pt[:], reads=[pr], writes=[oTr])
                yield
                y, yr = py.next()
                xt, xr = L['x']
                G, SH, GA, mr = mod
                for half in range(2):
                    for kc in range(KC):
                        S.mm(y[:, half * 512:(half + 1) * 512], oT[:, kc, :], wout[:, kc, half * 512:(half + 1) * 512],
                             kc == 0, kc == KC - 1, reads=[oTr, woutr], writes=[yr])
                for half in range(2):
                    tmp, tmpr = RS[s]['tmp'].next()
                    S.tt('vector', tmp[:], y[:, half * 512:(half + 1) * 512], GA[:, half * 512:(half + 1) * 512], ALU.mult,
                         reads=[yr, mr], writes=[tmpr])
                    S.tt('gpsimd', xt[:, half * 512:(half + 1) * 512], tmp[:], xt[:, half * 512:(half + 1) * 512],
                         ALU.add, reads=[tmpr, xr], writes=[xr])
                yield
                S.dma('sync', self.x_dst(s, t), xt[:], reads=[xr])
                self.written.add((s, t))

            def seq(s):
                Sst, Sr = states[s][0], states[s][1]
                S.memset('vector', Sst[:], 0.0, writes=[Sr])
                order = list(range(NTC - 1, -1, -1)) + list(range(NT - 1, NTC - 1, -1))
                Ls = {0: loads(s, order[0])}
                if len(order) > 1:
                    Ls[1] = loads(s, order[1])
                yield from chain_part(s, order[0], Ls[0])
                for i, t in enumerate(order):
                    if i + 2 < len(order):
                        Ls[i + 2] = loads(s, order[i + 2])
                    gens = [readout_part(s, t, Ls[i])]
                    if i + 1 < len(order):
                        gens.append(chain_part(s, order[i + 1], Ls[i + 1]))
                    yield from ileave(gens)
                    del Ls[i]

            self.fill_mod(modC, layer, 2, 0, self.I['norm1_g'], None, None)
            for s in range(cfg.NS):
                self.fill_mod(modSs[s], layer, s, 0, self.I['norm1_g'], None, None)
            for _ in ileave([seq(s) for s in range(cfg.NS)]):
                pass
            S.flush()


def ileave(gens, weights=None):
    items = [[g, (weights[i] if weights else 1)] for i, g in enumerate(gens)]
    while items:
        for it in list(items):
            for _ in range(it[1]):
                try:
                    next(it[0])
                except StopIteration:
                    items.remove(it)
                    break
                yield


_CACHE = {}


def run(inputs, cfg, ncores):
    key = (cfg.L, cfg.CTX, cfg.depth, cfg.NS, cfg.do_mixer, cfg.do_ffn)
    if key not in _CACHE:
        _CACHE[key] = Builder(cfg).build()
    nc = _CACHE[key]
    consts = host_consts(cfg)
    in_maps = []
    for i in range(ncores):
        m = {}
        sl = slice(i * cfg.NS, (i + 1) * cfg.NS)
        for k in input_shapes(cfg):
            a = np.asarray(inputs[k], dtype=np.float32)
            if k in ('x', 'c', 'ctx'):
                a = a[sl]
            m[k] = np.ascontiguousarray(a)
        for k, v in consts.items():
            m['k_' + k] = v
        in_maps.append(m)
    res = run_bass_kernel_spmd(nc, in_maps, core_ids=list(range(ncores)))
    if getattr(cfg, 'debug', False):
        global DEBUG_OUT
        DEBUG_OUT = res.results
    return np.concatenate([np.asarray(r["out"]) for r in res.results], axis=0).astype(np.float32)


def kernel(**inputs):
    cfg = Cfg()
    return run(inputs, cfg, 8)
```
